# Optimizing a Trainium2 kernel written in Bass

```python
import math
import jax, jax.numpy as jnp
from jax import lax
import numpy as np

D_MODEL = 2048
BATCH = 16
SEQ = 256
DEPTH = 1
DEC_BATCH = 8
DEC_SEQ = 2048
PAST_LEN = 512

GRID_W = 64
D_ATTN = D_MODEL // 2
N_HEADS = 8
HEAD_DIM = 64
D_POOL = D_MODEL - D_ATTN
POOL_WINDOWS = (2, 4, 8, 16)
N_POOL_GROUPS = 4
POOL_GROUP_DIM = D_POOL // N_POOL_GROUPS
D_IN = 3 * D_ATTN + D_POOL
D_FF = 5504
CONV_W = 3
ROT_AXIS = HEAD_DIM // 2
ROPE_BASE = 10000.0
Q_BLOCK = 128
N_MOD = 6
EPS = 1e-6

kernel_name = "hymba_diffattn_pool_convffn_prefix_dit"


def _rmsnorm(x, g):
    xf = x.astype(jnp.float32)
    y = xf * lax.rsqrt(jnp.mean(xf * xf, axis=-1, keepdims=True) + EPS)
    return (y * g.astype(jnp.float32)).astype(x.dtype)


def _adaln(cond, w, b):
    m = jax.nn.silu(cond) @ w + b
    return jnp.split(m[:, None, :], N_MOD, axis=-1)


def _in_proj(h, w_in):
    B, T = h.shape[0], h.shape[1]
    z = h @ w_in
    q, k, v, p = jnp.split(z, [D_ATTN, 2 * D_ATTN, 3 * D_ATTN], axis=-1)
    q = q.reshape(B, T, N_HEADS, 2, HEAD_DIM)
    k = k.reshape(B, T, N_HEADS, 2, HEAD_DIM)
    v = v.reshape(B, T, N_HEADS, 2 * HEAD_DIM)
    return q, k, v, p


def _lambda(lq1, lk1, lq2, lk2, lam_init):
    s1 = jnp.sum(lq1.astype(jnp.float32) * lk1.astype(jnp.float32))
    s2 = jnp.sum(lq2.astype(jnp.float32) * lk2.astype(jnp.float32))
    return jnp.exp(s1) - jnp.exp(s2) + lam_init


def _diff_core(q, k, v, lam):
    s = jnp.einsum("bqhmd,bkhmd->bhmqk", q, k, preferred_element_type=jnp.float32) * (HEAD_DIM ** -0.5)
    p = jax.nn.softmax(s, axis=-1)
    a = p[:, :, 0] - lam * p[:, :, 1]
    return jnp.einsum("bhqk,bkhe->bqhe", a.astype(v.dtype), v)


def _blocked_diff_attn(q, k_all, v_all, lam):
    B, T = q.shape[0], q.shape[1]
    nb = T // Q_BLOCK
    qb = q.reshape(B, nb, Q_BLOCK, N_HEADS, 2, HEAD_DIM).transpose(1, 0, 2, 3, 4, 5)
    ob = lax.map(lambda qq: _diff_core(qq, k_all, v_all, lam), qb)
    return ob.transpose(1, 0, 2, 3, 4).reshape(B, T, N_HEADS, 2 * HEAD_DIM)


def _axial_rope_tables(rows):
    row = jnp.repeat(jnp.arange(rows), GRID_W).astype(jnp.float32)
    col = jnp.tile(jnp.arange(GRID_W), rows).astype(jnp.float32)
    inv = 1.0 / (ROPE_BASE ** (jnp.arange(0, ROT_AXIS, 2, dtype=jnp.float32) / ROT_AXIS))
    ar = row[:, None] * inv
    ac = col[:, None] * inv
    ang = jnp.concatenate([ar, ar, ac, ac], axis=-1)
    return jnp.cos(ang), jnp.sin(ang)


def _rope(x, cos, sin):
    xr = x.reshape(*x.shape[:-1], 2, 2, ROT_AXIS // 2)
    rot = jnp.stack([-xr[..., 1, :], xr[..., 0, :]], axis=-2).reshape(x.shape)
    c = cos[None, :, None, None, :]
    s = sin[None, :, None, None, :]
    return (x.astype(jnp.float32) * c + rot.astype(jnp.float32) * s).astype(x.dtype)


def _centred_pool_minus_x(p, w):
    B, T, C = p.shape
    pf = p.astype(jnp.float32)
    S = jnp.concatenate([jnp.zeros((B, 1, C), jnp.float32), jnp.cumsum(pf, axis=1)], axis=1)
    t = jnp.arange(T)
    lo = jnp.clip(t - w // 2, 0, T)
    hi = jnp.clip(t + w - w // 2, 0, T)
    cnt = (hi - lo).astype(jnp.float32)
    mean = (S[:, hi] - S[:, lo]) / cnt[None, :, None]
    return (mean - pf).astype(p.dtype)


def _pool_mixer(p, w_pool, pool_scale):
    B, T = p.shape[0], p.shape[1]
    pg = p.reshape(B, T, N_POOL_GROUPS, POOL_GROUP_DIM)
    pooled = jnp.stack([_centred_pool_minus_x(pg[:, :, g], w) for g, w in enumerate(POOL_WINDOWS)], axis=2)
    out = jnp.einsum("btgc,gcd->btgd", pooled, w_pool)
    return out.reshape(B, T, D_POOL) * pool_scale


def _conv_ffn(h, w_up, conv_k, conv_b, w_down):
    T = h.shape[1]
    u = h @ w_up
    up = jnp.pad(u, ((0, 0), (1, 1), (0, 0)))
    u = up[:, :T] * conv_k[0] + up[:, 1:T + 1] * conv_k[1] + up[:, 2:] * conv_k[2] + conv_b
    gate, val = jnp.split(u, 2, axis=-1)
    return (jax.nn.silu(gate) * val) @ w_down


def _layer(x, cond, attend, w_ada, b_ada, norm1_g, w_in, subln_g, lam_init,
           w_pool, pool_scale, w_out, norm2_g, w_up, conv_k, conv_b, w_down):
    B, T = x.shape[0], x.shape[1]
    sh1, sc1, g1, sh2, sc2, g2 = _adaln(cond, w_ada, b_ada)
    h = _rmsnorm(x, norm1_g) * (1.0 + sc1) + sh1
    q, k, v, p = _in_proj(h, w_in)
    a = attend(q, k, v)
    a = (_rmsnorm(a, subln_g) * (1.0 - lam_init)).reshape(B, T, D_ATTN)
    m = _pool_mixer(p, w_pool, pool_scale)
    x = x + g1 * (jnp.concatenate([a, m], axis=-1) @ w_out)
    h2 = _rmsnorm(x, norm2_g) * (1.0 + sc2) + sh2
    x = x + g2 * _conv_ffn(h2, w_up, conv_k, conv_b, w_down)
    return x, k, v


def setup_inputs(seed: int = 0) -> dict:
    key = jax.random.key(seed)
    ks = jax.random.split(key, 26)
    f32 = jnp.float32
    nrm = lambda k, shape, s: jax.random.normal(k, shape, f32) * s
    return {
        "x_prompt": nrm(ks[0], (BATCH, SEQ, D_MODEL), 1.0),
        "x_sample": nrm(ks[1], (DEC_BATCH, DEC_SEQ, D_MODEL), 1.0),
        "c": nrm(ks[2], (DEC_BATCH, D_MODEL), 1.0),
        "cache_k": nrm(ks[3], (DEC_BATCH, DEPTH, PAST_LEN, N_HEADS, 2 * HEAD_DIM), 1.0),
        "cache_v": nrm(ks[4], (DEC_BATCH, DEPTH, PAST_LEN, N_HEADS, 2 * HEAD_DIM), 1.0),
        "c_ctx": nrm(ks[5], (D_MODEL,), 1.0),
        "w_ada": nrm(ks[6], (DEPTH, D_MODEL, N_MOD * D_MODEL), 0.5 * D_MODEL ** -0.5),
        "b_ada": nrm(ks[7], (DEPTH, N_MOD * D_MODEL), 0.01),
        "norm1_g": 1.0 + nrm(ks[8], (DEPTH, D_MODEL), 0.01),
        "w_in": nrm(ks[9], (DEPTH, D_MODEL, D_IN), D_MODEL ** -0.5),
        "lam_q1": nrm(ks[10], (DEPTH, HEAD_DIM), 0.1),
        "lam_k1": nrm(ks[11], (DEPTH, HEAD_DIM), 0.1),
        "lam_q2": nrm(ks[12], (DEPTH, HEAD_DIM), 0.1),
        "lam_k2": nrm(ks[13], (DEPTH, HEAD_DIM), 0.1),
        "subln_g": 1.0 + nrm(ks[14], (DEPTH, 2 * HEAD_DIM), 0.01),
        "w_pool": nrm(ks[15], (DEPTH, N_POOL_GROUPS, POOL_GROUP_DIM, POOL_GROUP_DIM), POOL_GROUP_DIM ** -0.5),
        "pool_scale": 1.0 + nrm(ks[16], (DEPTH, D_POOL), 0.1),
        "w_out": nrm(ks[17], (DEPTH, D_MODEL, D_MODEL), D_MODEL ** -0.5),
        "norm2_g": 1.0 + nrm(ks[18], (DEPTH, D_MODEL), 0.01),
        "w_up": nrm(ks[19], (DEPTH, D_MODEL, 2 * D_FF), D_MODEL ** -0.5),
        "conv_k": nrm(ks[20], (DEPTH, CONV_W, 2 * D_FF), CONV_W ** -0.5),
        "conv_b": nrm(ks[21], (DEPTH, 2 * D_FF), 0.01),
        "w_down": nrm(ks[22], (DEPTH, D_FF, D_MODEL), D_FF ** -0.5),
        "norm_f_g": 1.0 + nrm(ks[23], (D_MODEL,), 0.01),
    }


def reference(x_prompt, x_sample, c, cache_k, cache_v, c_ctx, w_ada, b_ada, norm1_g, w_in,
              lam_q1, lam_k1, lam_q2, lam_k2, subln_g, w_pool, pool_scale, w_out, norm2_g,
              w_up, conv_k, conv_b, w_down, norm_f_g):
    xc = x_prompt
    Bc, Lc = xc.shape[0], xc.shape[1]
    ks_new, vs_new = [], []
    for l in range(DEPTH):
        lam_init = 0.8 - 0.6 * math.exp(-0.3 * l)
        lam = _lambda(lam_q1[l], lam_k1[l], lam_q2[l], lam_k2[l], lam_init)
        attend_ctx = lambda q, k, v, lam=lam: _diff_core(q, k, v, lam)
        xc, k, v = _layer(xc, c_ctx[None, :], attend_ctx, w_ada[l], b_ada[l], norm1_g[l], w_in[l],
                          subln_g[l], lam_init, w_pool[l], pool_scale[l], w_out[l], norm2_g[l],
                          w_up[l], conv_k[l], conv_b[l], w_down[l])
        ks_new.append(k.reshape(Bc, Lc, N_HEADS, 2 * HEAD_DIM))
        vs_new.append(v)
    y_prompt = _rmsnorm(xc, norm_f_g)
    state_k = jnp.stack(ks_new, axis=1)
    state_v = jnp.stack(vs_new, axis=1)

    xs = x_sample
    T = xs.shape[1]
    rows = T // GRID_W
    cos, sin = _axial_rope_tables(rows)
    for l in range(DEPTH):
        lam_init = 0.8 - 0.6 * math.exp(-0.3 * l)
        lam = _lambda(lam_q1[l], lam_k1[l], lam_q2[l], lam_k2[l], lam_init)
        kc = cache_k[:, l]
        vc = cache_v[:, l]

        def attend_lat(q, k, v, kc=kc, vc=vc, lam=lam):
            q = _rope(q, cos, sin)
            k = _rope(k, cos, sin)
            kc5 = kc.reshape(kc.shape[0], kc.shape[1], N_HEADS, 2, HEAD_DIM)
            k_all = jnp.concatenate([kc5, k], axis=1)
            v_all = jnp.concatenate([vc, v], axis=1)
            return _blocked_diff_attn(q, k_all, v_all, lam)

        xs, _, _ = _layer(xs, c, attend_lat, w_ada[l], b_ada[l], norm1_g[l], w_in[l],
                          subln_g[l], lam_init, w_pool[l], pool_scale[l], w_out[l], norm2_g[l],
                          w_up[l], conv_k[l], conv_b[l], w_down[l])
    y_sample = _rmsnorm(xs, norm_f_g)
    return (y_prompt, y_sample, state_k, state_v)
```

```python
import math
import numpy as np
from contextlib import ExitStack
import concourse.bass as bass
import concourse.mybir as mybir
from concourse.bass_utils import run_bass_kernel_spmd

F32 = mybir.dt.float32
BF16 = mybir.dt.bfloat16
AF = mybir.ActivationFunctionType
ALU = mybir.AluOpType
AX = mybir.AxisListType

class Buf:
    __slots__ = ("name", "w", "r", "excl")

    def __init__(self, name, excl=False):
        self.name = name
        self.w = None
        self.r = {}
        self.excl = excl


class Op:
    __slots__ = ("eng", "fn", "deps", "dma", "sig", "sem", "val", "name")


ENGS = ("pe", "act", "dve", "pool", "sp")


class Sched:
    def __init__(self, nc, es, n_dma_sems=12, same_engine_sync=True):
        self.nc = nc
        self.es = es
        self.streams = {e: [] for e in ENGS}
        self.same_engine_sync = same_engine_sync
        self.eng_sem = {e: es.enter_context(nc.semaphore("sem_" + e)) for e in ENGS}
        self.dma_sems = {}
        self.dma_hist = {}
        self.n_dma_sems = n_dma_sems
        for q in ("sp", "pool", "act"):
            self.dma_sems[q] = [es.enter_context(nc.semaphore("dsem_%s_%d" % (q, i)))
                                for i in range(n_dma_sems)]
            self.dma_hist[q] = []
        self.nops = 0

    def add(self, eng, fn, reads=(), writes=(), dma=False, name=""):
        op = Op()
        op.eng = eng
        op.fn = fn
        op.dma = dma
        op.sig = False
        op.sem = None
        op.val = None
        op.name = name
        deps = []
        for b in reads:
            if b.w is not None:
                deps.append(b.w)
            if b.excl:
                for k, o in b.r.items():
                    if o.eng != eng or o.dma:
                        deps.append(o)
        for b in writes:
            if b.w is not None:
                deps.append(b.w)
            deps.extend(b.r.values())
        if dma:
            hist = self.dma_hist[eng]
            j = len(hist)
            op.sem = self.dma_sems[eng][j % self.n_dma_sems]
            op.val = 16 * (j // self.n_dma_sems + 1)
            if j >= self.n_dma_sems:
                deps.append(hist[j - self.n_dma_sems])
            hist.append(op)
        for b in reads:
            if dma:
                b.r[("dma", id(op))] = op
            else:
                b.r[eng] = op
        for b in writes:
            b.w = op
            b.r = {}
        seen = set()
        op.deps = []
        for d in deps:
            if d is op or id(d) in seen:
                continue
            seen.add(id(d))
            op.deps.append(d)
        self.streams[eng].append(op)
        self.nops += 1
        return op

    def barrier(self):
        last = {}
        for e in ENGS:
            for op in reversed(self.streams[e]):
                if not op.dma and op.fn is not None:
                    last[e] = op
                    break
        dmas = []
        for q in self.dma_hist:
            dmas.extend(self.dma_hist[q][-self.n_dma_sems:])
        for e in ENGS:
            op = Op()
            op.eng = e
            op.fn = None
            op.dma = False
            op.sig = False
            op.sem = None
            op.val = None
            op.name = "barrier"
            op.deps = [o for k, o in last.items() if k != e] + list(dmas)
            self.streams[e].append(op)

    def finalize_and_emit(self, block):
        for e in ENGS:
            for op in self.streams[e]:
                for d in op.deps:
                    if d.dma:
                        continue
                    if d.eng == op.eng and not op.dma:
                        if d.eng == "pe" or not self.same_engine_sync:
                            continue
                    d.sig = True
        for e in ENGS:
            cnt = 0
            for op in self.streams[e]:
                if op.dma:
                    continue
                if op.sig:
                    cnt += 1
                    op.sem = self.eng_sem[e]
                    op.val = cnt
        sched = self

        def emit(eng_name, eng):
            waited = {}
            for op in sched.streams[eng_name]:
                for d in op.deps:
                    if not d.dma:
                        if d.eng == op.eng and not op.dma and (d.eng == "pe" or not sched.same_engine_sync):
                            continue
                    key = id(d.sem)
                    if waited.get(key, 0) >= d.val:
                        continue
                    waited[key] = d.val
                    eng.wait_ge(d.sem, d.val)
                if op.fn is None:
                    continue
                ins = op.fn(eng)
                if op.dma:
                    ins.then_inc(op.sem, 16)
                elif op.sig:
                    ins.then_inc(op.sem, 1)

        @block.tensor
        def _(eng):
            emit("pe", eng)

        @block.scalar
        def _(eng):
            emit("act", eng)

        @block.vector
        def _(eng):
            emit("dve", eng)

        @block.gpsimd
        def _(eng):
            emit("pool", eng)

        @block.sync
        def _(eng):
            emit("sp", eng)

D = 2048
NTOK = 2560
DFF = 5504
NJ = 43
EPS = 1e-6
LAM_INIT = 0.8 - 0.6 * math.exp(0.0)
QSCALE = 64 ** -0.5
POOLW = (2, 4, 8, 16)


def build_program():
    nc = bass.Bass("TRN2", target_bir_lowering=False)

    def din(name, shape, dt=F32):
        return nc.dram_tensor(name, list(shape), dt, kind="ExternalInput").ap()

    def dout(name, shape):
        return nc.dram_tensor(name, list(shape), F32, kind="ExternalOutput").ap()

    xs = din("xs", [2048, D]); xp = din("xp", [512, D]); cT = din("cT", [128, 2, 16])
    ck = din("ck", [512, 1024]); cv = din("cv", [512, 1024])
    w_ada = din("w_ada", [D, 6 * D]); b_ada = din("b_ada", [1, 6 * D]); norm1_g = din("norm1_g", [1, D])
    w_in = din("w_in", [D, 4096]); perm_d = din("perm", [128, 128])
    lamv = din("lamv", [1, 256])
    subln_g = din("subln_g", [1, 128]); w_pool = din("w_pool", [4, 256, 256]); pool_scale = din("pool_scale", [1, 1024])
    w_out = din("w_out", [D, D]); norm2_g = din("norm2_g", [1, D]); w_up = din("w_up", [D, 2 * DFF])
    conv_k = din("conv_k", [3, 2 * DFF]); conv_b = din("conv_b", [1, 2 * DFF]); w_down = din("w_down", [DFF, D])
    norm_f_g = din("norm_f_g", [1, D])
    ident_d = din("ident", [128, 128]); cos_d = din("cosT", [128, 2048]); sin_d = din("sinT", [128, 2048])
    edge_d = din("edge", [1, 64])
    ys = dout("ys", [2048, D]); yp = dout("yp", [512, D]); sk = dout("sk", [512, 1024]); sv = dout("sv", [512, 1024])
    cat_scr = nc.dram_tensor("cat_scr", [16, 128, NTOK], BF16, kind="Internal").ap()
    x1_scr = nc.dram_tensor("x1_scr", [NTOK, D], F32, kind="Internal").ap()
    g_scr = nc.dram_tensor("g_scr", [4, D], F32, kind="Internal").ap()
    h2_scr = nc.dram_tensor("h2_scr", [16, 128, NTOK], BF16, kind="Internal").ap()
    wo_scr = nc.dram_tensor("wo_scr", [128, 16, D], BF16, kind="Internal").ap()
    wup_scr = nc.dram_tensor("wup_scr", [NJ, 128, 16, 2, 128], BF16, kind="Internal").ap()
    wdn_scr = nc.dram_tensor("wdn_scr", [4, 128, NJ, 512], BF16, kind="Internal").ap()

    with ExitStack() as es:
        S = Sched(nc, es)
        add = S.add
        cnt = [0]

        def sbuf(scope, shape, dt, name=None):
            cnt[0] += 1
            nm = "%s_%d" % (name or "t", cnt[0])
            return scope.enter_context(nc.sbuf_tensor(nm, list(shape), dt)), Buf(nm)

        pball = es.enter_context(nc.psum_tensor("pball", [128, 8, 512], F32))
        pb = [pball[:, i, :] for i in range(8)]
        b_pb = [Buf("pb%d" % i, excl=True) for i in range(8)]
        b_out = Buf("out")
        b_cat = Buf("cat_scr"); b_x1 = Buf("x1_scr"); b_g = Buf("g_scr"); b_h2s = Buf("h2_scr")
        b_wups = [[Buf("wups%d_%d" % (j, t)) for t in range(2)] for j in range(NJ)]
        b_wdns = [[Buf("wdns%d_%d" % (oc, fg)) for fg in range(4)] for oc in range(4)]
        w_up_v = w_up.rearrange("(kc p) (t f) -> p kc t f", p=128, t=2)
        w_down_v = w_down.rearrange("(fc p) n -> p fc n", p=128)
        precast = []
        for j in range(NJ):
            for t in range(2):
                precast.append((lambda e, j=j, t=t: e.dma_start(out=wup_scr[j, :, :, t, :], in_=w_up_v[:, :, t, j * 128:(j + 1) * 128]), b_wups[j][t]))
        for oc in range(4):
            for fg in range(4):
                f0 = fg * 11
                nf = min(11, NJ - f0)
                precast.append((lambda e, oc=oc, f0=f0, nf=nf: e.dma_start(out=wdn_scr[oc, :, f0:f0 + nf, :], in_=w_down_v[:, f0:f0 + nf, oc * 512:(oc + 1) * 512]),
                                b_wdns[oc][fg]))
        b_wos = [Buf("wos%d" % oc) for oc in range(4)]
        w_out_v0 = w_out.rearrange("(kc p) n -> p kc n", p=128)
        for oc in range(4):
            precast.insert(oc, (lambda e, oc=oc: e.dma_start(out=wo_scr[:, :, oc * 512:(oc + 1) * 512], in_=w_out_v0[:, :, oc * 512:(oc + 1) * 512]), b_wos[oc]))
        pc_i = [0]

        def issue_precast(n):
            for _ in range(n):
                if pc_i[0] < len(precast):
                    fn, bb = precast[pc_i[0]]
                    pc_i[0] += 1
                    add("pool", fn, writes=[bb], dma=True)

        ident, b_ident = sbuf(es, [128, 128], F32, "ident")
        ones_b, b_ones_b = sbuf(es, [128, 128], BF16, "ones_b")
        ones_f, b_ones_f = sbuf(es, [128, 128], F32, "ones_f")
        zeros, b_zeros = sbuf(es, [128, 128], F32, "zeros")
        colsT, b_cols = sbuf(es, [128, 7, 128], F32, "colsT")
        AB, b_AB = sbuf(es, [128, 2, 2, 2, 16], F32, "AB")
        neglam, b_neglam = sbuf(es, [128, 1], F32, "neglam")
        gsub, b_gsub = sbuf(es, [128, 1], F32, "gsub")
        edge, b_edge = sbuf(es, [128, 64], F32, "edge")
        epsc, b_epsc = sbuf(es, [128, 1], F32, "epsc")

        add("sp", lambda e: e.dma_start(out=ident[:], in_=ident_d[:, :]), writes=[b_ident], dma=True)
        permT, b_perm = sbuf(es, [128, 128], F32, "perm")
        add("sp", lambda e: e.dma_start(out=permT[:], in_=perm_d[:, :]), writes=[b_perm], dma=True)
        add("sp", lambda e: e.dma_start(out=edge[:], in_=edge_d[0:1, :].partition_broadcast(128)), writes=[b_edge], dma=True)
        add("dve", lambda e: e.memset(ones_b[:], 1.0), writes=[b_ones_b])
        add("dve", lambda e: e.memset(ones_f[:], 1.0), writes=[b_ones_f])
        add("dve", lambda e: e.memset(zeros[:], 0.0), writes=[b_zeros])
        add("dve", lambda e: e.memset(epsc[:], EPS), writes=[b_epsc])

        def rstd_from_ss(ss, b_ss, n, inv_n):
            add("act", lambda e: e.activation(out=ss[0:n, :], in_=ss[0:n, :], func=AF.Ln, scale=inv_n, bias=epsc[0:n, 0:1]),
                reads=[b_ss, b_epsc], writes=[b_ss])
            add("act", lambda e: e.activation(out=ss[0:n, :], in_=ss[0:n, :], func=AF.Exp, scale=-0.5),
                reads=[b_ss], writes=[b_ss])

        def norm_tile_to_hT(xt, b_xt, n, xn, b_xn, ss, b_ss, norm_i, cond, dst, b_dst, col_fn, bank):
            norm_stats(xt, b_xt, n, xn, b_xn, ss, b_ss)
            norm_transposes(n, xn, b_xn, norm_i, cond, b_dst, col_fn, bank)

        def norm_stats(xt, b_xt, n, xn, b_xn, ss, b_ss):
            add("act", lambda e: e.activation(out=xn[0:n, :], in_=xt[0:n, :], func=AF.Square, accum_out=ss[0:n, 0:1]),
                reads=[b_xt], writes=[b_xn, b_ss])
            rstd_from_ss(ss, b_ss, n, 1.0 / D)
            add("dve", lambda e: e.tensor_scalar(out=xn[0:n, :], in0=xt[0:n, :], scalar1=ss[0:n, 0:1], scalar2=None, op0=ALU.mult),
                reads=[b_xt, b_ss], writes=[b_xn])

        def norm_transposes(n, xn, b_xn, norm_i, cond, b_dst, col_fn, bank_):
            for q in range(4):
                bank = bank_[q % 2] if isinstance(bank_, tuple) else bank_
                def tr(e, q=q, bank=bank):
                    ins = None
                    for i in range(4):
                        kc = q * 4 + i
                        ins = e.transpose(out=pb[bank][:, i * 128:i * 128 + n], in_=xn[0:n, kc * 128:(kc + 1) * 128],
                                          identity=ident[0:n, 0:n])
                    return ins
                add("pe", tr, reads=[b_xn, b_ident], writes=[b_pb[bank]])

                def ev(e, q=q, bank=bank):
                    ins = None
                    for i in range(4):
                        kc = q * 4 + i
                        ins = e.activation(out=col_fn(kc), in_=pb[bank][:, i * 128:i * 128 + n], func=AF.Identity,
                                           scale=AB[:, norm_i, cond, 0, kc:kc + 1], bias=AB[:, norm_i, cond, 1, kc:kc + 1])
                    return ins
                add("act", ev, reads=[b_pb[bank], b_AB], writes=[b_dst])

        hT_scope = ExitStack()
        hT_all, b_hT_all = sbuf(hT_scope, [128, 16, NTOK], BF16, "hT")
        ph0 = ExitStack()
        ph = ph0
        if True:
            stg, b_stg = sbuf(ph, [128, 5, 128], F32, "stg")
            add("dve", lambda e: e.memset(stg[:], 0.0), writes=[b_stg])
            rows = [
                (b_ada.rearrange("o (r c) -> (o r) c", c=128), 0, 0, 96),
                (norm1_g.rearrange("o (r c) -> (o r) c", c=128), 1, 0, 16),
                (norm2_g.rearrange("o (r c) -> (o r) c", c=128), 1, 16, 16),
                (pool_scale.rearrange("o (r c) -> (o r) c", c=128), 1, 32, 8),
                (conv_b.rearrange("o (r c) -> (o r) c", c=128), 1, 40, 86),
                (conv_k[0:1, :].rearrange("o (r c) -> (o r) c", c=128), 2, 0, 86),
                (conv_k[1:2, :].rearrange("o (r c) -> (o r) c", c=128), 3, 0, 86),
                (conv_k[2:3, :].rearrange("o (r c) -> (o r) c", c=128), 4, 0, 86),
            ]
            for (src, blk, r0, nr) in rows:
                add("sp", lambda e, src=src, blk=blk, r0=r0, nr=nr: e.dma_start(out=stg[r0:r0 + nr, blk, :], in_=src),
                    reads=[], writes=[b_stg], dma=True)
            for blk in range(5):
                add("pe", lambda e, blk=blk: e.transpose(out=pb[0][:, 0:128], in_=stg[:, blk, :], identity=ident[:]),
                    reads=[b_stg, b_ident], writes=[b_pb[0]])
                add("dve", lambda e, blk=blk: e.tensor_copy(out=colsT[:, blk, :], in_=pb[0][:, 0:128]),
                    reads=[b_pb[0]], writes=[b_cols])
            for (dstb, srcb) in ((5, 2), (6, 4)):
                add("dve", lambda e, dstb=dstb, srcb=srcb: e.tensor_scalar(out=colsT[:, dstb, :], in0=colsT[:, srcb, :], scalar1=-1.0, scalar2=None, op0=ALU.mult),
                    reads=[b_cols], writes=[b_cols])
            lq, b_lq = sbuf(ph, [128, 256], F32, "lq")
            lt, b_lt = sbuf(ph, [128, 128], F32, "lt")
            ls, b_ls = sbuf(ph, [128, 2], F32, "ls")
            add("sp", lambda e: e.dma_start(out=lq[:], in_=lamv[0:1, :].partition_broadcast(128)), writes=[b_lq], dma=True)
            add("sp", lambda e: e.dma_start(out=gsub[:], in_=subln_g.rearrange("o (p one) -> (o p) one", one=1)),
                writes=[b_gsub], dma=True)
            add("dve", lambda e: e.tensor_tensor(out=lt[:, 0:64], in0=lq[:, 0:64], in1=lq[:, 64:128], op=ALU.mult),
                reads=[b_lq], writes=[b_lt])
            add("dve", lambda e: e.tensor_tensor(out=lt[:, 64:128], in0=lq[:, 128:192], in1=lq[:, 192:256], op=ALU.mult),
                reads=[b_lq], writes=[b_lt])
            add("dve", lambda e: e.reduce_sum(out=ls[:, 0:1], in_=lt[:, 0:64], axis=AX.X), reads=[b_lt], writes=[b_ls])
            add("dve", lambda e: e.reduce_sum(out=ls[:, 1:2], in_=lt[:, 64:128], axis=AX.X), reads=[b_lt], writes=[b_ls])
            add("act", lambda e: e.activation(out=ls[:], in_=ls[:], func=AF.Exp), reads=[b_ls], writes=[b_ls])
            add("dve", lambda e: e.tensor_tensor(out=neglam[:], in0=ls[:, 1:2], in1=ls[:, 0:1], op=ALU.subtract),
                reads=[b_ls], writes=[b_neglam])
            add("dve", lambda e: e.tensor_scalar(out=neglam[:], in0=neglam[:], scalar1=-LAM_INIT, scalar2=None, op0=ALU.add),
                reads=[b_neglam], writes=[b_neglam])
            add("dve", lambda e: e.tensor_scalar(out=gsub[:], in0=gsub[:], scalar1=1.0 - LAM_INIT, scalar2=None, op0=ALU.mult),
                reads=[b_gsub], writes=[b_gsub])
            cf, b_cf = sbuf(ph, [128, 2, 16], F32, "cf")
            cbf, b_cbf = sbuf(ph, [128, 16, 2], BF16, "cbf")
            crep, b_crep = sbuf(ph, [128, 2, 16, 128], BF16, "crep")
            add("sp", lambda e: e.dma_start(out=cf[:], in_=cT[:, :, :]), writes=[b_cf], dma=True)
            add("act", lambda e: e.activation(out=cf[:], in_=cf[:], func=AF.Silu), reads=[b_cf], writes=[b_cf])
            for c in range(2):
                add("dve", lambda e, c=c: e.tensor_copy(out=cbf[:, :, c], in_=cf[:, c, :]), reads=[b_cf], writes=[b_cbf])

                def rep(e, c=c):
                    ins = None
                    for kc in range(16):
                        ins = e.activation(out=crep[:, c, kc, :], in_=zeros[:], func=AF.Identity, bias=cf[:, c, kc:kc + 1])
                    return ins
                add("act", rep, reads=[b_cf, b_zeros], writes=[b_crep])
            modT, b_modT = sbuf(ph, [128, 96, 2], F32, "modT")
            brow, b_brow = sbuf(ph, [1, 2, 2048], F32, "brow")
            grow, b_grow = sbuf(ph, [128, 256], F32, "grow")
            add("sp", lambda e: e.dma_start(out=brow[0:1, 0, :], in_=b_ada[0:1, 2 * D:3 * D]), writes=[b_brow], dma=True)
            add("sp", lambda e: e.dma_start(out=brow[0:1, 1, :], in_=b_ada[0:1, 5 * D:6 * D]), writes=[b_brow], dma=True)
            wa = [sbuf(ph, [128, 16, 512], BF16, "wa") for _ in range(3)]
            grows = [sbuf(ph, [128, 512], F32, "grow2") for _ in range(2)]
            w_ada_v = w_ada.rearrange("(p kc) n -> p kc n", kc=16)
            MODBANK = 7
            first_mod = [True]
            def ada_chunk(nb):
                wt, b_wt = wa[nb % 3]
                add("pool", lambda e, wt=wt, nb=nb: e.dma_start(out=wt[:], in_=w_ada_v[:, :, nb * 512:(nb + 1) * 512]),
                    writes=[b_wt], dma=True)
                v = nb // 4
                if v in (0, 1, 3, 4):
                    def mm(e, wt=wt, nb=nb):
                        ins = None
                        for sub in range(4):
                            for kc in range(16):
                                ins = e.matmul(pb[1][:, sub * 2:sub * 2 + 2], lhsT=wt[:, kc, sub * 128:(sub + 1) * 128],
                                               rhs=cbf[:, kc, :], start=(kc == 0 and sub == 0), stop=(kc == 15),
                                               skip_group_check=True)
                        return ins
                    add("pe", mm, reads=[b_wt, b_cbf], writes=[b_pb[1]])
                    ch0 = nb * 4
                    for c in range(2):
                        add("dve", lambda e, c=c, ch0=ch0: e.tensor_tensor(
                            out=modT[:, ch0:ch0 + 4, c], in0=pb[1][:, c:8:2], in1=colsT[:, 0, ch0:ch0 + 4], op=ALU.add),
                            reads=[b_pb[1], b_cols], writes=[b_modT])
                else:
                    gi0 = 0 if v == 2 else 2
                    col0 = (nb % 4) * 512
                    for c in range(2):
                        bank = c
                        def mmg(e, wt=wt, c=c, bank=bank):
                            ins = None
                            for kc in range(16):
                                ins = e.matmul(pb[bank], lhsT=crep[:, c, kc, :], rhs=wt[:, kc, :],
                                               start=(kc == 0), stop=(kc == 15))
                            return ins
                        add("pe", mmg, reads=[b_wt, b_crep], writes=[b_pb[bank]])
                        gr, b_gr = grows[c]
                        add("dve", lambda e, bank=bank, v=v, col0=col0, gr=gr: e.tensor_tensor(
                            out=gr[0:1, :], in0=pb[bank][0:1, :], in1=brow[0:1, 0 if v == 2 else 1, col0:col0 + 512], op=ALU.add),
                            reads=[b_pb[bank], b_brow], writes=[b_gr])
                        add("sp", lambda e, gi=gi0 + c, col0=col0, gr=gr: e.dma_start(out=g_scr[gi:gi + 1, col0:col0 + 512], in_=gr[0:1, :]),
                            reads=[b_gr], writes=[b_g], dma=True)

            def ada_AB(ni):
                sh0, sc0, g0 = ((0, 16, 0), (48, 64, 16))[ni]
                for c in range(2):
                    add("dve", lambda e, c=c: e.scalar_tensor_tensor(
                        out=AB[:, ni, c, 0, :], in0=modT[:, sc0:sc0 + 16, c], scalar=1.0, in1=colsT[:, 1, g0:g0 + 16],
                        op0=ALU.add, op1=ALU.mult), reads=[b_modT, b_cols], writes=[b_AB])
                    add("dve", lambda e, c=c: e.tensor_copy(out=AB[:, ni, c, 1, :], in_=modT[:, sh0:sh0 + 16, c]),
                        reads=[b_modT], writes=[b_AB])

            for nb in range(8):
                ada_chunk(nb)
            ada_AB(0)
            ada_pending = list(range(8, 24))

        def run_ada(n):
            for _ in range(n):
                if ada_pending:
                    ada_chunk(ada_pending.pop(0))

        head_hook = [lambda: None]

        def xrows(tok):
            return (xs, tok) if tok < 2048 else (xp, tok - 2048)

        def stage_a(hT, b_hT, n_tok, segs3, a1_hook, after_a1, after_a1_free):
            tok0 = 0
            cache = True
            rope = True
            segs = [(s0_, l_) for (s0_, l_, c_) in segs3]
            nt = n_tok // 128
            nsub = n_tok // 512
            nkeys_max = (512 if cache else 0) + max(l for (_, l) in segs)
            with ExitStack() as ph:
                with ExitStack() as ph1:
                    xts = [sbuf(ph1, [128, D], F32, "xt") for _ in range(2)]
                    xn, b_xn = sbuf(ph1, [128, D], F32, "xn")
                    ss, b_ss = sbuf(ph1, [128, 1], F32, "ss")
                    for tt in range(nt):
                        xt, b_xt = xts[tt % 2]
                        src_, r0_ = xrows(tt * 128)
                        add("sp", lambda e, xt=xt, src_=src_, r0_=r0_: e.dma_start(out=xt[:], in_=src_[r0_:r0_ + 128, :]),
                            writes=[b_xt], dma=True)
                        norm_tile_to_hT(xt, b_xt, 128, xn, b_xn, ss, b_ss, 0, 0 if tt < 16 else 1, hT, b_hT,
                                        lambda kc, tt=tt: hT[:, kc, tt * 128:(tt + 1) * 128], (6, 7))
                        a1_hook()
                    after_a1()
                    S.barrier()
                after_a1_free()
                phh = ExitStack()
                wq = [sbuf(phh, [128, 16, 3, 128], BF16, "wq") for _ in range(2)]
                qf, b_qf = sbuf(phh, [128, 512], F32, "qf")
                qT, b_qT = sbuf(phh, [128, n_tok // 256, 512], BF16, "qT")
                add("dve", lambda e: e.memset(qT[:], 0.0), writes=[b_qT])
                kT, b_kT = sbuf(phh, [128, (512 if cache else 0) + n_tok], BF16, "kT")
                vT, b_vT = sbuf(phh, [128, 512], F32, "vT")
                vtok, b_vtok = sbuf(phh, [128, (4 if cache else 0) + nt, 128], BF16, "vtok")
                kf, b_kf = sbuf(phh, [128, 512], F32, "kf")
                kvo = [sbuf(phh, [128, 4, 128], F32, "kvo") for _ in range(2)]
                kcst, b_kcst = sbuf(phh, [128, 4, 128], F32, "kcst")
                t1, b_t1 = sbuf(phh, [128, 512], F32, "t1")
                t2, b_t2 = sbuf(phh, [128, 512], F32, "t2")
                PTs = [sbuf(phh, [128, 512], BF16, "PT") for _ in range(4)]
                rl, b_rl = sbuf(phh, [128, 512], F32, "rl")
                OL, b_OL = sbuf(phh, [128, 2, 512], F32, "OL")
                tn, b_tn = sbuf(phh, [128, 512], F32, "tn")
                oTs = [sbuf(phh, [128, 256], F32, "oT") for _ in range(2)]
                sqs = [sbuf(phh, [128, 256], F32, "sq") for _ in range(2)]
                fin_i = [0]
                rs, b_rs = sbuf(phh, [128, 256], F32, "rs")
                cst = [sbuf(phh, [128, n_tok], BF16, "cst") for _ in range(2)]
                if rope:
                    cosT, b_cos = sbuf(phh, [128, 2048], F32, "cos")
                    sinT, b_sin = sbuf(phh, [128, 2048], F32, "sin")
                    add("sp", lambda e: e.dma_start(out=cosT[:], in_=cos_d[:, :]), writes=[b_cos], dma=True)
                    add("sp", lambda e: e.dma_start(out=sinT[:], in_=sin_d[:, :]), writes=[b_sin], dma=True)
                koff = 512 if cache else 0
                wq_b = [[Buf("wqb") for _ in range(3)] for _ in range(2)]
                wqp_b = [[Buf("wqpb") for _ in range(2)] for _ in range(2)]
                w_in_v = w_in.rearrange("(kc p) (t h c) -> p kc t h c", p=128, t=4, h=8)
                pt_i = [0]
                st_i = [0]
                pj_i = [0]
                PROJ_BANKS = [0, 2, 3, 4, 5]
                pend = [None]
                catp = [None]
                def load_kc(hd):
                    add("sp", lambda e: e.dma_start(out=kcst[:], in_=ck.rearrange("(t p) f -> p t f", p=128)[:, :, hd * 128:(hd + 1) * 128]),
                        writes=[b_kcst], dma=True)

                def load_head_w(hd):
                    wt_, _b = wq[hd % 2]
                    for t in range(3):
                        add("pool", lambda e, t=t: e.dma_start(out=wt_[:, :, t, :], in_=w_in_v[:, :, t, hd, :]), writes=[wq_b[hd % 2][t]], dma=True)

                fin2_holder = [None]

                def flush_head():
                    if pend[0] is not None:
                        fin2_holder[0](pend[0])
                        pend[0] = None
                    if catp[0] is not None:
                        cs_, b_cs_, hd_ = catp[0]
                        catp[0] = None
                        add("sp", lambda e: e.dma_start(out=cat_scr[hd_, :, tok0:tok0 + n_tok], in_=cs_[:]), reads=[b_cs_], writes=[b_cat], dma=True)

                load_head_w(0)
                for hd in range(8):
                    add("pool", lambda e, hd=hd: e.dma_start(out=vtok[:, 0:4, :], in_=cv.rearrange("(t p) f -> p t f", p=128)[:, :, hd * 128:(hd + 1) * 128]),
                        writes=[b_vtok], dma=True)
                    if hd + 1 < 8:
                        load_head_w(hd + 1)
                    head_hook[0]()
                    wt, _b = wq[hd % 2]
                    b_wts = wq_b[hd % 2]
                    if cache:
                        if hd == 0:
                            load_kc(0)

                        def ktr(e):
                            ins = None
                            for i in range(4):
                                ins = e.transpose(out=pb[1][:, i * 128:(i + 1) * 128], in_=kcst[:, i, :], identity=ident[:])
                            return ins
                        add("pe", ktr, reads=[b_kcst, b_ident], writes=[b_pb[1]])
                        if hd + 1 < 8:
                            load_kc(hd + 1)
                        add("act", lambda e: e.activation(out=kT[:, 0:512], in_=pb[1][:, :], func=AF.Copy), reads=[b_pb[1]], writes=[b_kT])
                    for t, dst, b_dst, doff in ((0, qT, b_qT, 0), (1, kT, b_kT, koff)):
                        for sub in range(nsub):
                            c0 = sub * 512
                            bk = PROJ_BANKS[pj_i[0] % 5]
                            pj_i[0] += 1

                            def mm(e, bk=bk, wt=wt, t=t, c0=c0):
                                ins = None
                                for kc in range(16):
                                    ins = e.matmul(pb[bk][:, :], lhsT=wt[:, kc, t, :], rhs=hT[:, kc, c0:c0 + 512], start=(kc == 0), stop=(kc == 15))
                                return ins
                            add("pe", mm, reads=[b_wts[t], b_hT], writes=[b_pb[bk]])
                            if t == 0 and sub == 3:
                                flush_head()
                            if sub < 4:
                                add("act", lambda e, bk=bk: e.activation(out=qf[:], in_=pb[bk], func=AF.Copy), reads=[b_pb[bk]], writes=[b_qf])
                                add("pe", lambda e, bk=bk: e.matmul(pb[1], lhsT=permT[:, :], rhs=qf[:, :], start=True, stop=True),
                                    reads=[b_qf, b_perm], writes=[b_pb[1]])
                                add("dve", lambda e, bk=bk, c0=c0: e.tensor_tensor(out=t1[:], in0=pb[bk][:, :], in1=cosT[:, c0:c0 + 512], op=ALU.mult),
                                    reads=[b_pb[bk], b_cos], writes=[b_t1])
                                add("dve", lambda e, bk=bk, c0=c0: e.tensor_tensor(out=t2[:], in0=pb[1][:, :], in1=sinT[:, c0:c0 + 512], op=ALU.mult),
                                    reads=[b_pb[1], b_sin], writes=[b_t2])
                                if t == 0:
                                    def qw(e, bk=bk, c0=c0):
                                        s2 = c0 // 256
                                        e.tensor_tensor(out=qT[0:64, s2:s2 + 2, 0:256], in0=t1[0:64, :].rearrange("p (s q) -> p s q", s=2),
                                                        in1=t2[0:64, :].rearrange("p (s q) -> p s q", s=2), op=ALU.add)
                                        return e.tensor_tensor(out=qT[64:128, s2:s2 + 2, 256:512], in0=t1[64:128, :].rearrange("p (s q) -> p s q", s=2),
                                                               in1=t2[64:128, :].rearrange("p (s q) -> p s q", s=2), op=ALU.add)
                                    add("dve", qw, reads=[b_t1, b_t2], writes=[b_qT])
                                else:
                                    add("dve", lambda e, bk=bk, dst=dst, doff=doff, c0=c0: e.tensor_tensor(out=dst[:, doff + c0:doff + c0 + 512], in0=t1[:], in1=t2[:], op=ALU.add),
                                        reads=[b_t1, b_t2], writes=[b_dst])
                            else:
                                if t == 0:
                                    def qw2(e, bk=bk, c0=c0):
                                        s2 = c0 // 256
                                        e.activation(out=qT[0:64, s2:s2 + 2, 0:256], in_=pb[bk][0:64, :].rearrange("p (s q) -> p s q", s=2), func=AF.Copy)
                                        return e.activation(out=qT[64:128, s2:s2 + 2, 256:512], in_=pb[bk][64:128, :].rearrange("p (s q) -> p s q", s=2), func=AF.Copy)
                                    add("act", qw2, reads=[b_pb[bk]], writes=[b_qT])
                                else:
                                    add("act", lambda e, bk=bk, dst=dst, doff=doff, c0=c0: e.activation(out=dst[:, doff + c0:doff + c0 + 512], in_=pb[bk][:, :], func=AF.Copy),
                                        reads=[b_pb[bk]], writes=[b_dst])
                                if sub == 4 and t == 1:
                                    add("dve", lambda e, bk=bk: e.tensor_copy(out=kf[:], in_=pb[bk][:, :]), reads=[b_pb[bk]], writes=[b_kf])
                                    ko, b_ko = kvo[0]

                                    def ktr2(e, bk=bk):
                                        ins = None
                                        for i in range(4):
                                            ins = e.transpose(out=pb[1][:, i * 128:(i + 1) * 128], in_=kf[:, i * 128:(i + 1) * 128], identity=ident[:])
                                        return ins
                                    add("pe", ktr2, reads=[b_kf, b_ident], writes=[b_pb[1]])
                                    add("act", lambda e, bk=bk, ko=ko: e.activation(out=ko[:].rearrange("p a b -> p (a b)"), in_=pb[1][:, :], func=AF.Copy),
                                        reads=[b_pb[1]], writes=[b_ko])
                                    add("sp", lambda e, bk=bk, ko=ko, hd=hd, c0=c0: e.dma_start(
                                        out=sk[c0 - 2048:c0 - 2048 + 512, hd * 128:(hd + 1) * 128].rearrange("(t p) f -> p t f", p=128), in_=ko[:]),
                                        reads=[b_ko, b_out], dma=True)
                    for sub in range(nsub):
                        c0 = sub * 512
                        bk = PROJ_BANKS[pj_i[0] % 5]
                        pj_i[0] += 1

                        def mmv(e, bk=bk, wt=wt, c0=c0):
                            ins = None
                            for kc in range(16):
                                ins = e.matmul(pb[bk][:, :], lhsT=wt[:, kc, 2, :], rhs=hT[:, kc, c0:c0 + 512], start=(kc == 0), stop=(kc == 15))
                            return ins
                        add("pe", mmv, reads=[b_wts[2], b_hT], writes=[b_pb[bk]])
                        add("act", lambda e, bk=bk, c0=c0: e.activation(out=vT[:, :], in_=pb[bk][:, :], func=AF.Copy), reads=[b_pb[bk]], writes=[b_vT])

                        def vtr(e, bk=bk, c0=c0):
                            ins = None
                            for i in range(4):
                                ins = e.transpose(out=pb[1][:, i * 128:(i + 1) * 128], in_=vT[:, i * 128:(i + 1) * 128], identity=ident[:])
                            return ins
                        add("pe", vtr, reads=[b_vT, b_ident], writes=[b_pb[1]])
                        kt0 = (4 if cache else 0) + sub * 4
                        add("act", lambda e, bk=bk, kt0=kt0: e.activation(out=vtok[:, kt0:kt0 + 4, :].rearrange("p a b -> p (a b)"), in_=pb[1][:, :], func=AF.Copy),
                            reads=[b_pb[1]], writes=[b_vtok])
                        if sub == 4:
                            vo, b_vo = kvo[1]
                            add("dve", lambda e, bk=bk, vo=vo: e.tensor_copy(out=vo[:].rearrange("p a b -> p (a b)"), in_=pb[1][:, :]), reads=[b_pb[1]], writes=[b_vo])
                            add("sp", lambda e, bk=bk, vo=vo, hd=hd, c0=c0: e.dma_start(
                                out=sv[c0 - 2048:c0 - 2048 + 512, hd * 128:(hd + 1) * 128].rearrange("(t p) f -> p t f", p=128), in_=vo[:]),
                                reads=[b_vo, b_out], dma=True)
                    cs, b_cs = cst[hd % 2]
                    def fin2(args):
                        o_, b_o, q_, b_q, q0, cs_, b_cs_ = args
                        add("pe", lambda e: e.matmul(pb[1][:, 0:256], lhsT=ones_f[:, :], rhs=q_[:, :], start=True, stop=True),
                            reads=[b_q, b_ones_f], writes=[b_pb[1]])
                        add("act", lambda e: e.activation(out=rs[:], in_=pb[1][:, 0:256], func=AF.Ln, scale=1.0 / 128, bias=epsc[:, 0:1]),
                            reads=[b_pb[1], b_epsc], writes=[b_rs])
                        add("act", lambda e: e.activation(out=rs[:], in_=rs[:], func=AF.Exp, scale=-0.5), reads=[b_rs], writes=[b_rs])
                        add("dve", lambda e: e.scalar_tensor_tensor(out=cs_[:, q0:q0 + 256], in0=o_[:], scalar=gsub[:, 0:1], in1=rs[:],
                                                                   op0=ALU.mult, op1=ALU.mult), reads=[b_o, b_rs, b_gsub], writes=[b_cs_])
                    fin2_holder[0] = fin2

                    for (s0, slen, segc) in segs3:
                        nkc = ((512 if segc else 0) + slen) // 128
                        for qs in range(slen // 256):
                            q0 = s0 + qs * 256
                            qsi = q0 // 256
                            its = []
                            for kc in range(nkc):
                                if segc:
                                    kcol = kc * 128 if kc < 4 else koff + s0 + (kc - 4) * 128
                                    vti = kc if kc < 4 else 4 + (s0 // 128) + (kc - 4)
                                else:
                                    kcol = koff + s0 + kc * 128
                                    vti = 4 + (s0 // 128) + kc
                                sA = 2 + (st_i[0] % 4)
                                st_i[0] += 1
                                PT, b_PT = PTs[pt_i[0] % 4]
                                pt_i[0] += 1
                                its.append((kc, kcol, vti, sA, PT, b_PT))

                            def emit_qk(it, qsi=qsi):
                                kc, kcol, vti, sA, PT, b_PT = it
                                add("pe", lambda e: e.matmul(pb[sA], lhsT=kT[:, kcol:kcol + 128], rhs=qT[:, qsi, :], start=True, stop=True),
                                    reads=[b_kT, b_qT], writes=[b_pb[sA]])

                            def emit_ex_pv(it, nxt=None, nkc=nkc, qsi=qsi):
                                kc, kcol, vti, sA, PT, b_PT = it
                                add("act", lambda e: e.activation(out=PT[:, :], in_=pb[sA], func=AF.Exp, scale=QSCALE), reads=[b_pb[sA]], writes=[b_PT])

                                def pv(e):
                                    e.matmul(pb[6], lhsT=vtok[:, vti, :], rhs=PT[:, :], start=(kc == 0), stop=(kc == nkc - 1))
                                    ins = e.matmul(pb[7], lhsT=ones_b[:, :], rhs=PT[:, :], start=(kc == 0), stop=(kc == nkc - 1))
                                    if nxt is not None:
                                        ins = e.matmul(pb[nxt[3]], lhsT=kT[:, nxt[1]:nxt[1] + 128], rhs=qT[:, qsi, :], start=True, stop=True)
                                    return ins
                                wr = [b_pb[6], b_pb[7]] + ([b_pb[nxt[3]]] if nxt is not None else [])
                                add("pe", pv, reads=[b_PT, b_vtok, b_ones_b, b_kT, b_qT], writes=wr)

                            LA = 3
                            for ii in range(min(LA, len(its))):
                                emit_qk(its[ii])
                            for ii in range(len(its)):
                                emit_ex_pv(its[ii], its[ii + LA] if ii + LA < len(its) else None)
                                if ii == min(12, len(its) - 1) and pend[0] is not None:
                                    fin2(pend[0])
                                    pend[0] = None
                            if pend[0] is not None:
                                fin2(pend[0])
                                pend[0] = None
                            o_, b_o = oTs[fin_i[0] % 2]
                            q_, b_q = sqs[fin_i[0] % 2]
                            fin_i[0] += 1
                            add("dve", lambda e: e.tensor_copy(out=OL[:, :, :], in_=pball[:, 6:8, :]), reads=[b_pb[6], b_pb[7]], writes=[b_OL])
                            add("dve", lambda e: e.reciprocal(out=rl[:], in_=OL[:, 1, :]), reads=[b_OL], writes=[b_rl])
                            add("dve", lambda e: e.tensor_tensor(out=tn[:], in0=OL[:, 0, :], in1=rl[:], op=ALU.mult), reads=[b_OL, b_rl], writes=[b_tn])
                            add("dve", lambda e, o_=o_: e.scalar_tensor_tensor(out=o_[:], in0=tn[:, 256:512], scalar=neglam[:, 0:1], in1=tn[:, 0:256],
                                                                              op0=ALU.mult, op1=ALU.add), reads=[b_tn, b_neglam], writes=[b_o])
                            add("dve", lambda e, o_=o_, q_=q_: e.tensor_tensor(out=q_[:], in0=o_[:], in1=o_[:], op=ALU.mult), reads=[b_o], writes=[b_q])
                            pend[0] = (o_, b_o, q_, b_q, q0, cs, b_cs)
                    catp[0] = (cs, b_cs, hd)
                flush_head()
                S.barrier()
                phh.close()
                with ExitStack() as ph2:
                    Wd = sum(l + 16 for (_, l) in segs)
                    pp = [sbuf(ph2, [128, Wd], F32, "pp") for _ in range(2)]
                    sa, b_sa = sbuf(ph2, [128, Wd], F32, "sa")
                    sb2, b_sb2 = sbuf(ph2, [128, Wd], F32, "sb")
                    pooled, b_pooled = sbuf(ph2, [128, 2, n_tok], BF16, "pooled")
                    e8, b_e8 = sbuf(ph2, [128, 8], F32, "e8")
                    wpl = [sbuf(ph2, [128, 16, 2, 128], BF16, "wpl") for _ in range(2)]
                    wpp = [sbuf(ph2, [128, 2, 256], BF16, "wpp") for _ in range(2)]
                    cst2 = [sbuf(ph2, [128, n_tok], BF16, "cst2") for _ in range(2)]
                    for (p_, b_p) in pp:
                        add("dve", lambda e, p_=p_: e.memset(p_[:], 0.0), writes=[b_p])
                    w_in_p = w_in.rearrange("(kc p) (t g c) -> p kc t g c", p=128, t=4, g=8)
                    wpl_b = [[Buf("wplb") for _ in range(2)] for _ in range(2)]
                    def load_pool_w(g):
                        wt_, _b = wpl[g % 2]
                        for chl in range(2):
                            add("pool", lambda e, chl=chl: e.dma_start(out=wt_[:, :, chl, :], in_=w_in_p[:, :, 3, 2 * g + chl, :]), writes=[wpl_b[g % 2][chl]], dma=True)
                        wp__, b_wp_ = wpp[g % 2]
                        add("pool", lambda e: e.dma_start(out=wp__[:], in_=w_pool[g].rearrange("(kc p) n -> p kc n", p=128)), writes=[b_wp_], dma=True)

                    load_pool_w(0)
                    for g in range(4):
                        w = POOLW[g]
                        wt, _b = wpl[g % 2]
                        b_wls = wpl_b[g % 2]
                        wp_, b_wp = wpp[g % 2]
                        if g + 1 < 4:
                            load_pool_w(g + 1)
                        for chl in range(2):
                            p_, b_p = pp[chl]
                            for sub in range(nsub):
                                c0 = sub * 512

                                def mmp2(e, wt=wt, chl=chl, c0=c0):
                                    ins = None
                                    for kc in range(16):
                                        ins = e.matmul(pb[0][:, :], lhsT=wt[:, kc, chl, :], rhs=hT[:, kc, c0:c0 + 512], start=(kc == 0), stop=(kc == 15))
                                    return ins
                                add("pe", mmp2, reads=[b_wls[chl], b_hT], writes=[b_pb[0]])
                                off = 0
                                for (s0, slen) in segs:
                                    lo = max(s0, c0); hi = min(s0 + slen, c0 + 512)
                                    if lo < hi:
                                        po = off + 8 + (lo - s0)
                                        add("act", lambda e, p_=p_, po=po, lo=lo, hi=hi, c0=c0: e.activation(
                                            out=p_[:, po:po + hi - lo], in_=pb[0][:, lo - c0:hi - c0], func=AF.Copy),
                                            reads=[b_pb[0]], writes=[b_p])
                                    off += slen + 16
                            srcs = [(p_, b_p), (sa, b_sa), (sb2, b_sb2)]
                            cur, b_cur = p_, b_p
                            sh = 1
                            k = 0
                            while sh < w:
                                dstt, b_dstt = (sa, b_sa) if k % 2 == 0 else (sb2, b_sb2)
                                nout = Wd - (2 * sh - 1)
                                add("dve", lambda e, cur=cur, dstt=dstt, sh=sh, nout=nout: e.tensor_tensor(
                                    out=dstt[:, 0:nout], in0=cur[:, 0:nout], in1=cur[:, sh:nout + sh], op=ALU.add),
                                    reads=[b_cur], writes=[b_dstt])
                                cur, b_cur = dstt, b_dstt
                                sh *= 2
                                k += 1
                            off = 0
                            for (s0, slen) in segs:
                                u0 = off + 8
                                h2_ = w // 2
                                add("dve", lambda e, cur=cur, p_=p_, u0=u0, slen=slen, s0=s0, chl=chl, w=w, h2_=h2_: e.scalar_tensor_tensor(
                                    out=pooled[:, chl, s0:s0 + slen], in0=cur[:, u0 - h2_:u0 - h2_ + slen], scalar=1.0 / w, in1=p_[:, u0:u0 + slen],
                                    op0=ALU.mult, op1=ALU.subtract), reads=[b_cur, b_p], writes=[b_pooled])
                                for side in range(2):
                                    ub = u0 if side == 0 else u0 + slen - 8
                                    tb = s0 if side == 0 else s0 + slen - 8
                                    ecol = g * 16 + side * 8
                                    add("dve", lambda e, cur=cur, ub=ub, h2_=h2_, ecol=ecol: e.tensor_tensor(
                                        out=e8[:], in0=cur[:, ub - h2_:ub - h2_ + 8], in1=edge[:, ecol:ecol + 8], op=ALU.mult),
                                        reads=[b_cur, b_edge], writes=[b_e8])
                                    add("dve", lambda e, p_=p_, ub=ub, tb=tb, chl=chl: e.tensor_tensor(
                                        out=pooled[:, chl, tb:tb + 8], in0=e8[:], in1=p_[:, ub:ub + 8], op=ALU.subtract),
                                        reads=[b_e8, b_p], writes=[b_pooled])
                                off += slen + 16
                        for oc2 in range(2):
                            cs, b_cs = cst2[oc2]
                            for sub in range(nsub):
                                c0 = sub * 512

                                def mmw(e, wp_=wp_, oc2=oc2, c0=c0):
                                    ins = None
                                    for kc2 in range(2):
                                        ins = e.matmul(pb[1][:, :], lhsT=wp_[:, kc2, oc2 * 128:(oc2 + 1) * 128], rhs=pooled[:, kc2, c0:c0 + 512],
                                                       start=(kc2 == 0), stop=(kc2 == 1))
                                    return ins
                                add("pe", mmw, reads=[b_wp, b_pooled], writes=[b_pb[1]])
                                pc = 32 + 2 * g + oc2
                                add("act", lambda e, cs=cs, c0=c0, pc=pc: e.activation(out=cs[:, c0:c0 + 512], in_=pb[1][:, :], func=AF.Identity,
                                                                                     scale=colsT[:, 1, pc:pc + 1]), reads=[b_pb[1], b_cols], writes=[b_cs])
                            add("sp", lambda e, cs=cs, g=g, oc2=oc2: e.dma_start(out=cat_scr[8 + 2 * g + oc2, :, tok0:tok0 + n_tok], in_=cs[:]),
                                reads=[b_cs], writes=[b_cat], dma=True)
                    S.barrier()
            S.barrier()

        def _after_a1():
            run_ada(100)
            ada_AB(1)

        head_hook[0] = lambda: issue_precast(14)
        stage_a(hT_all, b_hT_all, NTOK, [(0, 2048, True), (2048, 256, False), (2304, 256, False)],
                lambda: run_ada(1), _after_a1, lambda: ph0.close())
        hT_scope.close()

        issue_precast(1000)
        with ExitStack() as ph:
            catb = [sbuf(ph, [128, 16, 512], BF16, "catb") for _ in range(2)]
            h2st = [sbuf(ph, [128, 16, 512], BF16, "h2st") for _ in range(2)]
            wo, _b = sbuf(ph, [128, 16, D], BF16, "wo")
            b_wo = [Buf("wo%d" % oc) for oc in range(4)]
            xts = [sbuf(ph, [128, D], F32, "xt4") for _ in range(3)]
            xn4s = [sbuf(ph, [128, D], F32, "xn4") for _ in range(2)]
            ss4s = [sbuf(ph, [128, 1], F32, "ss4") for _ in range(2)]
            g1r = [sbuf(ph, [128, D], F32, "g1r") for _ in range(2)]
            tmp = [sbuf(ph, [128, 512], F32, "tmp4") for _ in range(2)]
            def load_g1(c):
                add("act", lambda e: e.dma_start(out=g1r[c][0][:], in_=g_scr[c:c + 1, :].partition_broadcast(128)),
                    reads=[b_g], writes=[g1r[c][1]], dma=True)

            def load_wo(oc):
                add("act", lambda e: e.dma_start(out=wo[:, :, oc * 512:(oc + 1) * 512], in_=wo_scr[:, :, oc * 512:(oc + 1) * 512]),
                    reads=[b_wos[oc]], writes=[b_wo[oc]], dma=True)

            load_wo(0)
            load_g1(1)
            load_wo(1)
            load_wo(2)
            load_wo(3)
            load_g1(0)

            ti = 0
            mi = 0
            order = [4, 0, 1, 2, 3]
            tiles = [(blk, tt) for blk in order for tt in range(4)]

            def load_cat(oi):
                blk = order[oi]
                cb, b_cb = catb[oi % 2]
                t0 = blk * 512
                add("sp", lambda e: e.dma_start(out=cb[:], in_=cat_scr[:, :, t0:t0 + 512].rearrange("k p t -> p k t")),
                    reads=[b_cat], writes=[b_cb], dma=True)

            def load_x(n):
                blk, tt = tiles[n]
                xt, b_xt = xts[n % 3]
                src, r0 = xrows(blk * 512 + tt * 128)
                add("sp", lambda e: e.dma_start(out=xt[:], in_=src[r0:r0 + 128, :]), writes=[b_xt], dma=True)

            def do_stats(n):
                xt, b_xt = xts[n % 3]
                xn4, b_xn4 = xn4s[n % 2]
                ss4, b_ss4 = ss4s[n % 2]
                norm_stats(xt, b_xt, 128, xn4, b_xn4, ss4, b_ss4)

            def do_norm(n):
                blk, tt = tiles[n]
                oi = n // 4
                xn4, b_xn4 = xn4s[n % 2]
                hs, b_hs = h2st[oi % 2]
                cond = 0 if blk < 4 else 1
                norm_transposes(128, xn4, b_xn4, 1, cond, b_hs, lambda kc: hs[:, kc, tt * 128:(tt + 1) * 128], (6, 7))
                if tt == 3:
                    t0 = blk * 512
                    add("pool", lambda e: e.dma_start(out=h2_scr[:, :, t0:t0 + 512].rearrange("k p t -> p k t"), in_=hs[:]),
                        reads=[b_hs], writes=[b_h2s], dma=True)

            load_cat(0)
            load_x(0)
            load_x(1)
            for n, (blk, tt) in enumerate(tiles):
                oi = n // 4
                t0 = blk * 512
                cond = 0 if blk < 4 else 1
                cb, b_cb = catb[oi % 2]
                if tt == 0 and oi + 1 < 5:
                    load_cat(oi + 1)
                xt, b_xt = xts[n % 3]
                for oc in range(4):
                    bank = mi % 6
                    mi += 1

                    def mmo(e, cb=cb, tt=tt, bank=bank, oc=oc):
                        ins = None
                        for kc in range(16):
                            ins = e.matmul(pb[bank], lhsT=cb[:, kc, tt * 128:(tt + 1) * 128], rhs=wo[:, kc, oc * 512:(oc + 1) * 512],
                                           start=(kc == 0), stop=(kc == 15))
                        return ins
                    add("pe", mmo, reads=[b_cb, b_wo[oc]], writes=[b_pb[bank]])
                    tm, b_tm = tmp[ti % 2]
                    ti += 1
                    add("dve", lambda e, tm=tm, bank=bank, cond=cond, oc=oc: e.tensor_tensor(
                        out=tm[:], in0=pb[bank], in1=g1r[cond][0][:, oc * 512:(oc + 1) * 512], op=ALU.mult),
                        reads=[b_pb[bank], g1r[cond][1]], writes=[b_tm])
                    add("dve", lambda e, tm=tm, xt=xt, oc=oc: e.tensor_tensor(
                        out=xt[:, oc * 512:(oc + 1) * 512], in0=xt[:, oc * 512:(oc + 1) * 512], in1=tm[:], op=ALU.add),
                        reads=[b_tm, b_xt], writes=[b_xt])
                r1 = t0 + tt * 128
                add("pool", lambda e, xt=xt, r1=r1: e.dma_start(out=x1_scr[r1:r1 + 128, :], in_=xt[:]), reads=[b_xt], writes=[b_x1], dma=True)
                if n >= 1:
                    do_norm(n - 1)
                do_stats(n)
                if n + 2 < len(tiles):
                    load_x(n + 2)
            do_norm(len(tiles) - 1)
            S.barrier()

        with ExitStack() as ph:
            h2Ts = [sbuf(ph, [128, 16, 512], BF16, "h2T") for _ in range(2)]
            hH, b_hH = sbuf(ph, [128, 16, 6], BF16, "hH")
            uh_g, b_uhg = sbuf(ph, [128, NJ, 8], F32, "uh_g")
            uh_v, b_uhv = sbuf(ph, [128, NJ, 8], F32, "uh_v")
            gT, b_gT = sbuf(ph, [128, NJ, 512], BF16, "gT")
            wu = [sbuf(ph, [128, 16, 2, 128], BF16, "wu") for _ in range(2)]
            wd = [sbuf(ph, [128, 11, 512], BF16, "wd") for _ in range(2)]
            xts = [sbuf(ph, [128, D], F32, "xtb") for _ in range(4)]
            xn, b_xn = sbuf(ph, [128, D], BF16, "junkb")
            ss, b_ss = sbuf(ph, [128, 1], F32, "ssb")
            ug = [sbuf(ph, [128, 514], F32, "ug") for _ in range(2)]
            uv = [sbuf(ph, [128, 514], F32, "uv") for _ in range(2)]
            tg, b_tg = sbuf(ph, [128, 512], F32, "tg")
            tv, b_tv = sbuf(ph, [128, 512], F32, "tv")
            sg, b_sg = sbuf(ph, [128, 512], F32, "sg")
            fx, b_fx = sbuf(ph, [128, 1], F32, "fx")
            g2rs = [sbuf(ph, [128, D], F32, "g2r") for _ in range(2)]
            gfr, b_gfr = sbuf(ph, [128, D], F32, "gfr")
            tmp = [sbuf(ph, [128, 512], F32, "tmpb") for _ in range(2)]
            add("sp", lambda e: e.dma_start(out=gfr[:], in_=norm_f_g[0:1, :].partition_broadcast(128)), writes=[b_gfr], dma=True)
            for c in range(2):
                add("sp", lambda e, c=c: e.dma_start(out=g2rs[c][0][:], in_=g_scr[2 + c:3 + c, :].partition_broadcast(128)),
                    reads=[b_g], writes=[g2rs[c][1]], dma=True)
            add("dve", lambda e: e.memset(uh_g[:], 0.0), writes=[b_uhg])
            add("dve", lambda e: e.memset(uh_v[:], 0.0), writes=[b_uhv])
            HALO = [511, 512, 1023, 1024, 1535, 1536]
            for i in range(3):
                r0 = HALO[2 * i]
                add("sp", lambda e, r0=r0, i=i: e.dma_start(out=hH[:, :, 2 * i:2 * i + 2], in_=h2_scr[:, :, r0:r0 + 2].rearrange("k p t -> p k t")),
                    reads=[b_h2s], writes=[b_hH], dma=True)
            wdi = 0
            ti = 0
            blocks = [(2048, 1, 0, 0, [(0, 256), (256, 256)], yp, 0)]
            for i in range(4):
                t0 = i * 512
                hlft = 0 if i == 0 else 1 + HALO.index(t0 - 1)
                hrgt = 0 if i == 3 else 1 + HALO.index(t0 + 512)
                blocks.append((t0, 0, hlft, hrgt, [(0, 512)], ys, t0))

            def load_h2(bi):
                t0_ = blocks[bi][0]
                h_, b_h = h2Ts[bi % 2]
                add("sp", lambda e: e.dma_start(out=h_[:], in_=h2_scr[:, :, t0_:t0_ + 512].rearrange("k p t -> p k t")),
                    reads=[b_h2s], writes=[b_h], dma=True)

            load_h2(0)
            for bi, (t0, cond, hlft, hrgt, segs, ydst, y0) in enumerate(blocks):
                h2T, b_h2T = h2Ts[bi % 2]
                g2r, b_g2r = g2rs[cond]
                for tt in range(4):
                    xt, b_xt = xts[tt]
                    r0 = t0 + tt * 128
                    add("pool", lambda e, xt=xt, r0=r0: e.dma_start(out=xt[:], in_=x1_scr[r0:r0 + 128, :]), reads=[b_x1], writes=[b_xt], dma=True)
                for j in range(NJ):
                    if j == 24 and bi + 1 < len(blocks):
                        load_h2(bi + 1)
                    wt, b_wt1 = wu[j % 2]
                    add("sp", lambda e, wt=wt, j=j: e.dma_start(out=wt[:], in_=wup_scr[j]), reads=b_wups[j], writes=[b_wt1], dma=True)
                    st = 2 * (j % 3)
                    bG, bV, bH = st, st + 1, 7
                    u_g, b_ug = ug[j % 2]
                    u_v, b_uv = uv[j % 2]

                    def mmu(e, wt=wt, bG=bG, bV=bV, h2T=h2T):
                        ins = None
                        for kc in range(16):
                            e.matmul(pb[bG], lhsT=wt[:, kc, 0, :], rhs=h2T[:, kc, :], start=(kc == 0), stop=(kc == 15))
                        for kc in range(16):
                            ins = e.matmul(pb[bV], lhsT=wt[:, kc, 1, :], rhs=h2T[:, kc, :], start=(kc == 0), stop=(kc == 15))
                        return ins
                    add("pe", mmu, reads=[b_wt1, b_h2T], writes=[b_pb[bG], b_pb[bV]])
                    if bi == 0:
                        def mmh(e, wt=wt, bH=bH):
                            ins = None
                            for kc in range(16):
                                e.matmul(pb[bH][:, 0:6], lhsT=wt[:, kc, 0, :], rhs=hH[:, kc, :], start=(kc == 0), stop=(kc == 15))
                            for kc in range(16):
                                ins = e.matmul(pb[bH][:, 6:12], lhsT=wt[:, kc, 1, :], rhs=hH[:, kc, :], start=False, stop=(kc == 15),
                                               skip_group_check=True)
                            return ins
                        add("pe", mmh, reads=[b_wt1, b_hH], writes=[b_pb[bH]])

                        def evh(e, j=j, bH=bH):
                            e.activation(out=uh_g[:, j, 1:7], in_=pb[bH][:, 0:6], func=AF.Copy)
                            return e.activation(out=uh_v[:, j, 1:7], in_=pb[bH][:, 6:12], func=AF.Copy)
                        add("act", evh, reads=[b_pb[bH]], writes=[b_uhg, b_uhv])

                    def evu(e, u_g=u_g, u_v=u_v, bG=bG, bV=bV, j=j, hlft=hlft, hrgt=hrgt):
                        e.activation(out=u_g[:, 0:1], in_=uh_g[:, j, hlft:hlft + 1], func=AF.Copy)
                        e.activation(out=u_g[:, 513:514], in_=uh_g[:, j, hrgt:hrgt + 1], func=AF.Copy)
                        e.activation(out=u_v[:, 0:1], in_=uh_v[:, j, hlft:hlft + 1], func=AF.Copy)
                        e.activation(out=u_v[:, 513:514], in_=uh_v[:, j, hrgt:hrgt + 1], func=AF.Copy)
                        e.activation(out=u_g[:, 1:513], in_=pb[bG], func=AF.Copy)
                        return e.activation(out=u_v[:, 1:513], in_=pb[bV], func=AF.Copy)
                    add("act", evu, reads=[b_pb[bG], b_pb[bV], b_uhg, b_uhv], writes=[b_ug, b_uv])
                    for (u_, b_u, t_, b_t, cj) in ((u_g, b_ug, tg, b_tg, j), (u_v, b_uv, tv, b_tv, NJ + j)):
                        def conv(e, u_=u_, t_=t_, cj=cj, segs=segs):
                            e.tensor_scalar(out=t_[:], in0=u_[:, 1:513], scalar1=colsT[:, 3, cj:cj + 1], scalar2=colsT[:, 1, 40 + cj:41 + cj],
                                            op0=ALU.mult, op1=ALU.add)
                            e.scalar_tensor_tensor(out=t_[:], in0=u_[:, 0:512], scalar=colsT[:, 2, cj:cj + 1], in1=t_[:], op0=ALU.mult, op1=ALU.add)
                            ins = e.scalar_tensor_tensor(out=t_[:], in0=u_[:, 2:514], scalar=colsT[:, 4, cj:cj + 1], in1=t_[:], op0=ALU.mult, op1=ALU.add)
                            if len(segs) == 2:
                                e.scalar_tensor_tensor(out=t_[:, 255:256], in0=u_[:, 257:258], scalar=colsT[:, 6, cj:cj + 1], in1=t_[:, 255:256],
                                                       op0=ALU.mult, op1=ALU.add)
                                ins = e.scalar_tensor_tensor(out=t_[:, 256:257], in0=u_[:, 256:257], scalar=colsT[:, 5, cj:cj + 1], in1=t_[:, 256:257],
                                                             op0=ALU.mult, op1=ALU.add)
                            return ins
                        add("dve", conv, reads=[b_u, b_cols], writes=[b_t, b_fx])
                    add("act", lambda e: e.activation(out=sg[:], in_=tg[:], func=AF.Silu), reads=[b_tg], writes=[b_sg])
                    add("dve", lambda e, j=j: e.tensor_tensor(out=gT[:, j, :], in0=sg[:], in1=tv[:], op=ALU.mult), reads=[b_sg, b_tv], writes=[b_gT])
                for oc in range(4):
                    base = 4 * (oc % 2)
                    for fg in range(4):
                        f0 = fg * 11
                        nf = min(11, NJ - f0)
                        wt, b_wt = wd[wdi % 2]
                        wdi += 1
                        add("sp", lambda e, wt=wt, f0=f0, nf=nf, oc=oc: e.dma_start(out=wt[:, 0:nf, :], in_=wdn_scr[oc, :, f0:f0 + nf, :]),
                            reads=[b_wdns[oc][fg]], writes=[b_wt], dma=True)

                        def mmd(e, wt=wt, f0=f0, nf=nf, base=base):
                            ins = None
                            for fl in range(nf):
                                ffc = f0 + fl
                                for tt in range(4):
                                    ins = e.matmul(pb[base + tt][:, :], lhsT=gT[:, ffc, tt * 128:(tt + 1) * 128], rhs=wt[:, fl, :],
                                                   start=(ffc == 0), stop=(ffc == NJ - 1))
                            return ins
                        add("pe", mmd, reads=[b_wt, b_gT], writes=[b_pb[base + tt] for tt in range(4)])
                    for tt in range(4):
                        xt, b_xt = xts[tt]
                        tm, b_tm = tmp[ti % 2]
                        ti += 1
                        add("dve", lambda e, tm=tm, base=base, tt=tt, oc=oc, g2r=g2r: e.tensor_tensor(
                            out=tm[:], in0=pb[base + tt][:, :], in1=g2r[:, oc * 512:(oc + 1) * 512], op=ALU.mult),
                            reads=[b_pb[base + tt], b_g2r], writes=[b_tm])
                        add("dve", lambda e, tm=tm, xt=xt, oc=oc: e.tensor_tensor(
                            out=xt[:, oc * 512:(oc + 1) * 512], in0=xt[:, oc * 512:(oc + 1) * 512], in1=tm[:], op=ALU.add),
                            reads=[b_tm, b_xt], writes=[b_xt])
                for tt in range(4):
                    xt, b_xt = xts[tt]
                    add("act", lambda e, xt=xt: e.activation(out=xn[:], in_=xt[:], func=AF.Square, accum_out=ss[:, 0:1]), reads=[b_xt], writes=[b_xn, b_ss])
                    rstd_from_ss(ss, b_ss, 128, 1.0 / D)
                    add("dve", lambda e, xt=xt: e.scalar_tensor_tensor(out=xt[:], in0=xt[:], scalar=ss[:, 0:1], in1=gfr[:], op0=ALU.mult, op1=ALU.mult),
                        reads=[b_xt, b_ss, b_gfr], writes=[b_xt])
                    r0 = y0 + tt * 128
                    add("pool", lambda e, xt=xt, ydst=ydst, r0=r0: e.dma_start(out=ydst[r0:r0 + 128, :], in_=xt[:]), reads=[b_xt, b_out], dma=True)
            S.barrier()
        add("sp", None, writes=[b_out])
        with nc.Block() as block:
            S.finalize_and_emit(block)
    return nc


_CACHE = {}


def _consts():
    rows = 2048 // 64
    row = np.repeat(np.arange(rows), 64).astype(np.float32)
    col = np.tile(np.arange(64), rows).astype(np.float32)
    inv = (1.0 / (10000.0 ** (np.arange(0, 32, 2, dtype=np.float32) / np.float32(32)))).astype(np.float32)
    ar = row[:, None] * inv
    ac = col[:, None] * inv
    ang = np.concatenate([ar, ar, ac, ac], axis=-1).astype(np.float32)
    cos = np.cos(ang).astype(np.float32)
    sin = np.sin(ang).astype(np.float32)
    sgn = np.ones(64, np.float32)
    for a in range(2):
        sgn[a * 32:a * 32 + 16] = -1.0
    sins = sin * sgn[None, :]
    cosT = np.ascontiguousarray(np.concatenate([cos.T, cos.T], axis=0))
    sinT = np.ascontiguousarray(np.concatenate([sins.T, sins.T], axis=0))
    edge = np.zeros((1, 64), np.float32)
    for g, w in enumerate(POOLW):
        for i in range(8):
            t = i
            cl = (t + w - w // 2) - max(t - w // 2, 0)
            edge[0, g * 16 + i] = 1.0 / cl
            tr = -8 + i
            cr = min(0, tr + w - w // 2) - (tr - w // 2)
            edge[0, g * 16 + 8 + i] = 1.0 / cr
    return cosT, sinT, edge


def kernel(x_prompt, x_sample, c, cache_k, cache_v, c_ctx, w_ada, b_ada, norm1_g, w_in,
           lam_q1, lam_k1, lam_q2, lam_k2, subln_g, w_pool, pool_scale, w_out, norm2_g,
           w_up, conv_k, conv_b, w_down, norm_f_g):
    f = lambda a: np.ascontiguousarray(np.asarray(a, dtype=np.float32))
    x_prompt, x_sample, c, cache_k, cache_v, c_ctx = map(f, (x_prompt, x_sample, c, cache_k, cache_v, c_ctx))
    if "nc" not in _CACHE:
        _CACHE["nc"] = build_program()
    nc = _CACHE["nc"]
    cosT, sinT, edge = _consts()
    w_in0 = f(w_in)[0]
    perm = np.zeros((128, 128), np.float32)
    for p in range(128):
        perm[p ^ 16, p] = 1.0
    lamv = np.concatenate([f(lam_q1)[0], f(lam_k1)[0], f(lam_q2)[0], f(lam_k2)[0]])[None, :]
    shared = {
        "w_ada": f(w_ada)[0], "b_ada": f(b_ada), "norm1_g": f(norm1_g), "w_in": w_in0, "perm": perm,
        "lamv": f(lamv), "subln_g": f(subln_g), "w_pool": f(w_pool)[0], "pool_scale": f(pool_scale),
        "w_out": f(w_out)[0], "norm2_g": f(norm2_g), "w_up": f(w_up)[0], "conv_k": f(conv_k)[0], "conv_b": f(conv_b),
        "w_down": f(w_down)[0], "norm_f_g": f(norm_f_g)[None, :], "ident": np.eye(128, dtype=np.float32),
        "cosT": cosT, "sinT": sinT, "edge": edge,
    }
    in_maps = []
    for b in range(8):
        m = dict(shared)
        m["xs"] = x_sample[b]
        m["xp"] = np.ascontiguousarray(x_prompt[2 * b:2 * b + 2].reshape(512, 2048))
        cc = np.stack([c[b].reshape(128, 16), c_ctx.reshape(128, 16)], axis=1)
        m["cT"] = np.ascontiguousarray(cc)
        m["ck"] = np.ascontiguousarray(cache_k[b, 0].reshape(512, 1024))
        m["cv"] = np.ascontiguousarray(cache_v[b, 0].reshape(512, 1024))
        in_maps.append(m)
    res = run_bass_kernel_spmd(nc, in_maps, core_ids=list(range(8)))
    r = res.results
    y_prompt = np.concatenate([r[b]["yp"].reshape(2, 256, 2048) for b in range(8)], axis=0)
    y_sample = np.stack([r[b]["ys"] for b in range(8)], axis=0)
    state_k = np.concatenate([r[b]["sk"].reshape(2, 1, 256, 8, 128) for b in range(8)], axis=0)
    state_v = np.concatenate([r[b]["sv"].reshape(2, 1, 256, 8, 128) for b in range(8)], axis=0)
    return (y_prompt.astype(np.float32), y_sample.astype(np.float32), state_k.astype(np.float32), state_v.astype(np.float32))
```

```python
import math
import numpy as np
from contextlib import ExitStack
import concourse.bass as bass
import concourse.mybir as mybir
from concourse.bass_utils import run_bass_kernel_spmd

F32 = mybir.dt.float32
BF16 = mybir.dt.bfloat16
AF = mybir.ActivationFunctionType
ALU = mybir.AluOpType
AX = mybir.AxisListType

class Buf:
    __slots__ = ("name", "w", "r", "excl")

    def __init__(self, name, excl=False):
        self.name = name
        self.w = None
        self.r = {}
        self.excl = excl


class Op:
    __slots__ = ("eng", "fn", "deps", "dma", "sig", "sem", "val", "name")


ENGS = ("pe", "act", "dve", "pool", "sp")


class Sched:
    def __init__(self, nc, es, n_dma_sems=12, same_engine_sync=True):
        self.nc = nc
        self.es = es
        self.streams = {e: [] for e in ENGS}
        self.same_engine_sync = same_engine_sync
        self.eng_sem = {e: es.enter_context(nc.semaphore("sem_" + e)) for e in ENGS}
        self.dma_sems = {}
        self.dma_hist = {}
        self.n_dma_sems = n_dma_sems
        for q in ("sp", "pool", "act"):
            self.dma_sems[q] = [es.enter_context(nc.semaphore("dsem_%s_%d" % (q, i)))
                                for i in range(n_dma_sems)]
            self.dma_hist[q] = []
        self.nops = 0

    def add(self, eng, fn, reads=(), writes=(), dma=False, name=""):
        op = Op()
        op.eng = eng
        op.fn = fn
        op.dma = dma
        op.sig = False
        op.sem = None
        op.val = None
        op.name = name
        deps = []
        for b in reads:
            if b.w is not None:
                deps.append(b.w)
            if b.excl:
                for k, o in b.r.items():
                    if o.eng != eng or o.dma:
                        deps.append(o)
        for b in writes:
            if b.w is not None:
                deps.append(b.w)
            deps.extend(b.r.values())
        if dma:
            hist = self.dma_hist[eng]
            j = len(hist)
            op.sem = self.dma_sems[eng][j % self.n_dma_sems]
            op.val = 16 * (j // self.n_dma_sems + 1)
            if j >= self.n_dma_sems:
                deps.append(hist[j - self.n_dma_sems])
            hist.append(op)
        for b in reads:
            if dma:
                b.r[("dma", id(op))] = op
            else:
                b.r[eng] = op
        for b in writes:
            b.w = op
            b.r = {}
        seen = set()
        op.deps = []
        for d in deps:
            if d is op or id(d) in seen:
                continue
            seen.add(id(d))
            op.deps.append(d)
        self.streams[eng].append(op)
        self.nops += 1
        return op

    def barrier(self):
        last = {}
        for e in ENGS:
            for op in reversed(self.streams[e]):
                if not op.dma and op.fn is not None:
                    last[e] = op
                    break
        dmas = []
        for q in self.dma_hist:
            dmas.extend(self.dma_hist[q][-self.n_dma_sems:])
        for e in ENGS:
            op = Op()
            op.eng = e
            op.fn = None
            op.dma = False
            op.sig = False
            op.sem = None
            op.val = None
            op.name = "barrier"
            op.deps = [o for k, o in last.items() if k != e] + list(dmas)
            self.streams[e].append(op)

    def finalize_and_emit(self, block):
        for e in ENGS:
            for op in self.streams[e]:
                for d in op.deps:
                    if d.dma:
                        continue
                    if d.eng == op.eng and not op.dma:
                        if d.eng == "pe" or not self.same_engine_sync:
                            continue
                    d.sig = True
        for e in ENGS:
            cnt = 0
            for op in self.streams[e]:
                if op.dma:
                    continue
                if op.sig:
                    cnt += 1
                    op.sem = self.eng_sem[e]
                    op.val = cnt
        sched = self

        def emit(eng_name, eng):
            waited = {}
            for op in sched.streams[eng_name]:
                for d in op.deps:
                    if not d.dma:
                        if d.eng == op.eng and not op.dma and (d.eng == "pe" or not sched.same_engine_sync):
                            continue
                    key = id(d.sem)
                    if waited.get(key, 0) >= d.val:
                        continue
                    waited[key] = d.val
                    eng.wait_ge(d.sem, d.val)
                if op.fn is None:
                    continue
                ins = op.fn(eng)
                if op.dma:
                    ins.then_inc(op.sem, 16)
                elif op.sig:
                    ins.then_inc(op.sem, 1)

        @block.tensor
        def _(eng):
            emit("pe", eng)

        @block.scalar
        def _(eng):
            emit("act", eng)

        @block.vector
        def _(eng):
            emit("dve", eng)

        @block.gpsimd
        def _(eng):
            emit("pool", eng)

        @block.sync
        def _(eng):
            emit("sp", eng)

D = 2048
NTOK = 2560
DFF = 5504
NJ = 43
EPS = 1e-6
LAM_INIT = 0.8 - 0.6 * math.exp(0.0)
QSCALE = 64 ** -0.5
POOLW = (2, 4, 8, 16)


def build_program():
    nc = bass.Bass("TRN2", target_bir_lowering=False)

    def din(name, shape, dt=F32):
        return nc.dram_tensor(name, list(shape), dt, kind="ExternalInput").ap()

    def dout(name, shape):
        return nc.dram_tensor(name, list(shape), F32, kind="ExternalOutput").ap()

    xs = din("xs", [2048, D]); xp = din("xp", [512, D]); cT = din("cT", [128, 2, 16])
    ck = din("ck", [512, 1024]); cv = din("cv", [512, 1024])
    w_ada = din("w_ada", [D, 6 * D]); b_ada = din("b_ada", [1, 6 * D]); norm1_g = din("norm1_g", [1, D])
    w_in = din("w_in", [D, 4096]); perm_d = din("perm", [128, 128])
    lamv = din("lamv", [1, 256])
    subln_g = din("subln_g", [1, 128]); w_pool = din("w_pool", [4, 256, 256]); pool_scale = din("pool_scale", [1, 1024])
    w_out = din("w_out", [D, D]); norm2_g = din("norm2_g", [1, D]); w_up = din("w_up", [D, 2 * DFF])
    conv_k = din("conv_k", [3, 2 * DFF]); conv_b = din("conv_b", [1, 2 * DFF]); w_down = din("w_down", [DFF, D])
    norm_f_g = din("norm_f_g", [1, D])
    ident_d = din("ident", [128, 128]); cos_d = din("cosT", [128, 2048]); sin_d = din("sinT", [128, 2048])
    edge_d = din("edge", [1, 64])
    ys = dout("ys", [2048, D]); yp = dout("yp", [512, D]); sk = dout("sk", [512, 1024]); sv = dout("sv", [512, 1024])
    cat_scr = nc.dram_tensor("cat_scr", [16, 128, NTOK], BF16, kind="Internal").ap()
    x1_scr = nc.dram_tensor("x1_scr", [NTOK, D], F32, kind="Internal").ap()
    g_scr = nc.dram_tensor("g_scr", [4, D], F32, kind="Internal").ap()
    h2_scr = nc.dram_tensor("h2_scr", [16, 128, NTOK], BF16, kind="Internal").ap()
    wo_scr = nc.dram_tensor("wo_scr", [128, 16, D], BF16, kind="Internal").ap()
    wup_scr = nc.dram_tensor("wup_scr", [NJ, 128, 16, 2, 128], BF16, kind="Internal").ap()
    wdn_scr = nc.dram_tensor("wdn_scr", [4, 128, NJ, 512], BF16, kind="Internal").ap()

    with ExitStack() as es:
        S = Sched(nc, es)
        add = S.add
        cnt = [0]

        def sbuf(scope, shape, dt, name=None):
            cnt[0] += 1
            nm = "%s_%d" % (name or "t", cnt[0])
            return scope.enter_context(nc.sbuf_tensor(nm, list(shape), dt)), Buf(nm)

        pball = es.enter_context(nc.psum_tensor("pball", [128, 8, 512], F32))
        pb = [pball[:, i, :] for i in range(8)]
        b_pb = [Buf("pb%d" % i, excl=True) for i in range(8)]
        b_out = Buf("out")
        b_cat = Buf("cat_scr"); b_x1 = Buf("x1_scr"); b_g = Buf("g_scr"); b_h2s = Buf("h2_scr")
        b_wups = [[Buf("wups%d_%d" % (j, t)) for t in range(2)] for j in range(NJ)]
        b_wdns = [[Buf("wdns%d_%d" % (oc, fg)) for fg in range(4)] for oc in range(4)]
        w_up_v = w_up.rearrange("(kc p) (t f) -> p kc t f", p=128, t=2)
        w_down_v = w_down.rearrange("(fc p) n -> p fc n", p=128)
        precast = []
        for j in range(NJ):
            for t in range(2):
                precast.append((lambda e, j=j, t=t: e.dma_start(out=wup_scr[j, :, :, t, :], in_=w_up_v[:, :, t, j * 128:(j + 1) * 128]), b_wups[j][t]))
        for oc in range(4):
            for fg in range(4):
                f0 = fg * 11
                nf = min(11, NJ - f0)
                precast.append((lambda e, oc=oc, f0=f0, nf=nf: e.dma_start(out=wdn_scr[oc, :, f0:f0 + nf, :], in_=w_down_v[:, f0:f0 + nf, oc * 512:(oc + 1) * 512]),
                                b_wdns[oc][fg]))
        b_wos = [Buf("wos%d" % oc) for oc in range(4)]
        w_out_v0 = w_out.rearrange("(kc p) n -> p kc n", p=128)
        for oc in range(4):
            precast.insert(oc, (lambda e, oc=oc: e.dma_start(out=wo_scr[:, :, oc * 512:(oc + 1) * 512], in_=w_out_v0[:, :, oc * 512:(oc + 1) * 512]), b_wos[oc]))
        pc_i = [0]

        def issue_precast(n):
            for _ in range(n):
                if pc_i[0] < len(precast):
                    fn, bb = precast[pc_i[0]]
                    pc_i[0] += 1
                    add("pool", fn, writes=[bb], dma=True)

        ident, b_ident = sbuf(es, [128, 128], F32, "ident")
        ones_b, b_ones_b = sbuf(es, [128, 128], BF16, "ones_b")
        ones_f, b_ones_f = sbuf(es, [128, 128], F32, "ones_f")
        zeros, b_zeros = sbuf(es, [128, 128], F32, "zeros")
        colsT, b_cols = sbuf(es, [128, 7, 128], F32, "colsT")
        AB, b_AB = sbuf(es, [128, 2, 2, 2, 16], F32, "AB")
        neglam, b_neglam = sbuf(es, [128, 1], F32, "neglam")
        gsub, b_gsub = sbuf(es, [128, 1], F32, "gsub")
        edge, b_edge = sbuf(es, [128, 64], F32, "edge")
        epsc, b_epsc = sbuf(es, [128, 1], F32, "epsc")

        add("sp", lambda e: e.dma_start(out=ident[:], in_=ident_d[:, :]), writes=[b_ident], dma=True)
        permT, b_perm = sbuf(es, [128, 128], F32, "perm")
        add("sp", lambda e: e.dma_start(out=permT[:], in_=perm_d[:, :]), writes=[b_perm], dma=True)
        add("sp", lambda e: e.dma_start(out=edge[:], in_=edge_d[0:1, :].partition_broadcast(128)), writes=[b_edge], dma=True)
        add("dve", lambda e: e.memset(ones_b[:], 1.0), writes=[b_ones_b])
        add("dve", lambda e: e.memset(ones_f[:], 1.0), writes=[b_ones_f])
        add("dve", lambda e: e.memset(zeros[:], 0.0), writes=[b_zeros])
        add("dve", lambda e: e.memset(epsc[:], EPS), writes=[b_epsc])

        def rstd_from_ss(ss, b_ss, n, inv_n):
            add("act", lambda e: e.activation(out=ss[0:n, :], in_=ss[0:n, :], func=AF.Ln, scale=inv_n, bias=epsc[0:n, 0:1]),
                reads=[b_ss, b_epsc], writes=[b_ss])
            add("act", lambda e: e.activation(out=ss[0:n, :], in_=ss[0:n, :], func=AF.Exp, scale=-0.5),
                reads=[b_ss], writes=[b_ss])

        def norm_tile_to_hT(xt, b_xt, n, xn, b_xn, ss, b_ss, norm_i, cond, dst, b_dst, col_fn, bank):
            norm_stats(xt, b_xt, n, xn, b_xn, ss, b_ss)
            norm_transposes(n, xn, b_xn, norm_i, cond, b_dst, col_fn, bank)

        def norm_stats(xt, b_xt, n, xn, b_xn, ss, b_ss):
            add("act", lambda e: e.activation(out=xn[0:n, :], in_=xt[0:n, :], func=AF.Square, accum_out=ss[0:n, 0:1]),
                reads=[b_xt], writes=[b_xn, b_ss])
            rstd_from_ss(ss, b_ss, n, 1.0 / D)
            add("dve", lambda e: e.tensor_scalar(out=xn[0:n, :], in0=xt[0:n, :], scalar1=ss[0:n, 0:1], scalar2=None, op0=ALU.mult),
                reads=[b_xt, b_ss], writes=[b_xn])

        def norm_transposes(n, xn, b_xn, norm_i, cond, b_dst, col_fn, bank_):
            for q in range(4):
                bank = bank_[q % 2] if isinstance(bank_, tuple) else bank_
                def tr(e, q=q, bank=bank):
                    ins = None
                    for i in range(4):
                        kc = q * 4 + i
                        ins = e.transpose(out=pb[bank][:, i * 128:i * 128 + n], in_=xn[0:n, kc * 128:(kc + 1) * 128],
                                          identity=ident[0:n, 0:n])
                    return ins
                add("pe", tr, reads=[b_xn, b_ident], writes=[b_pb[bank]])

                def ev(e, q=q, bank=bank):
                    ins = None
                    for i in range(4):
                        kc = q * 4 + i
                        ins = e.activation(out=col_fn(kc), in_=pb[bank][:, i * 128:i * 128 + n], func=AF.Identity,
                                           scale=AB[:, norm_i, cond, 0, kc:kc + 1], bias=AB[:, norm_i, cond, 1, kc:kc + 1])
                    return ins
                add("act", ev, reads=[b_pb[bank], b_AB], writes=[b_dst])

        hT_scope = ExitStack()
        hT_all, b_hT_all = sbuf(hT_scope, [128, 16, NTOK], BF16, "hT")
        ph0 = ExitStack()
        ph = ph0
        if True:
            stg, b_stg = sbuf(ph, [128, 5, 128], F32, "stg")
            add("dve", lambda e: e.memset(stg[:], 0.0), writes=[b_stg])
            rows = [
                (b_ada.rearrange("o (r c) -> (o r) c", c=128), 0, 0, 96),
                (norm1_g.rearrange("o (r c) -> (o r) c", c=128), 1, 0, 16),
                (norm2_g.rearrange("o (r c) -> (o r) c", c=128), 1, 16, 16),
                (pool_scale.rearrange("o (r c) -> (o r) c", c=128), 1, 32, 8),
                (conv_b.rearrange("o (r c) -> (o r) c", c=128), 1, 40, 86),
                (conv_k[0:1, :].rearrange("o (r c) -> (o r) c", c=128), 2, 0, 86),
                (conv_k[1:2, :].rearrange("o (r c) -> (o r) c", c=128), 3, 0, 86),
                (conv_k[2:3, :].rearrange("o (r c) -> (o r) c", c=128), 4, 0, 86),
            ]
            for (src, blk, r0, nr) in rows:
                add("sp", lambda e, src=src, blk=blk, r0=r0, nr=nr: e.dma_start(out=stg[r0:r0 + nr, blk, :], in_=src),
                    reads=[], writes=[b_stg], dma=True)
            for blk in range(5):
                add("pe", lambda e, blk=blk: e.transpose(out=pb[0][:, 0:128], in_=stg[:, blk, :], identity=ident[:]),
                    reads=[b_stg, b_ident], writes=[b_pb[0]])
                add("dve", lambda e, blk=blk: e.tensor_copy(out=colsT[:, blk, :], in_=pb[0][:, 0:128]),
                    reads=[b_pb[0]], writes=[b_cols])
            for (dstb, srcb) in ((5, 2), (6, 4)):
                add("dve", lambda e, dstb=dstb, srcb=srcb: e.tensor_scalar(out=colsT[:, dstb, :], in0=colsT[:, srcb, :], scalar1=-1.0, scalar2=None, op0=ALU.mult),
                    reads=[b_cols], writes=[b_cols])
            lq, b_lq = sbuf(ph, [128, 256], F32, "lq")
            lt, b_lt = sbuf(ph, [128, 128], F32, "lt")
            ls, b_ls = sbuf(ph, [128, 2], F32, "ls")
            add("sp", lambda e: e.dma_start(out=lq[:], in_=lamv[0:1, :].partition_broadcast(128)), writes=[b_lq], dma=True)
            add("sp", lambda e: e.dma_start(out=gsub[:], in_=subln_g.rearrange("o (p one) -> (o p) one", one=1)),
                writes=[b_gsub], dma=True)
            add("dve", lambda e: e.tensor_tensor(out=lt[:, 0:64], in0=lq[:, 0:64], in1=lq[:, 64:128], op=ALU.mult),
                reads=[b_lq], writes=[b_lt])
            add("dve", lambda e: e.tensor_tensor(out=lt[:, 64:128], in0=lq[:, 128:192], in1=lq[:, 192:256], op=ALU.mult),
                reads=[b_lq], writes=[b_lt])
            add("dve", lambda e: e.reduce_sum(out=ls[:, 0:1], in_=lt[:, 0:64], axis=AX.X), reads=[b_lt], writes=[b_ls])
            add("dve", lambda e: e.reduce_sum(out=ls[:, 1:2], in_=lt[:, 64:128], axis=AX.X), reads=[b_lt], writes=[b_ls])
            add("act", lambda e: e.activation(out=ls[:], in_=ls[:], func=AF.Exp), reads=[b_ls], writes=[b_ls])
            add("dve", lambda e: e.tensor_tensor(out=neglam[:], in0=ls[:, 1:2], in1=ls[:, 0:1], op=ALU.subtract),
                reads=[b_ls], writes=[b_neglam])
            add("dve", lambda e: e.tensor_scalar(out=neglam[:], in0=neglam[:], scalar1=-LAM_INIT, scalar2=None, op0=ALU.add),
                reads=[b_neglam], writes=[b_neglam])
            add("dve", lambda e: e.tensor_scalar(out=gsub[:], in0=gsub[:], scalar1=1.0 - LAM_INIT, scalar2=None, op0=ALU.mult),
                reads=[b_gsub], writes=[b_gsub])
            cf, b_cf = sbuf(ph, [128, 2, 16], F32, "cf")
            cbf, b_cbf = sbuf(ph, [128, 16, 2], BF16, "cbf")
            crep, b_crep = sbuf(ph, [128, 2, 16, 128], BF16, "crep")
            add("sp", lambda e: e.dma_start(out=cf[:], in_=cT[:, :, :]), writes=[b_cf], dma=True)
            add("act", lambda e: e.activation(out=cf[:], in_=cf[:], func=AF.Silu), reads=[b_cf], writes=[b_cf])
            for c in range(2):
                add("dve", lambda e, c=c: e.tensor_copy(out=cbf[:, :, c], in_=cf[:, c, :]), reads=[b_cf], writes=[b_cbf])

                def rep(e, c=c):
                    ins = None
                    for kc in range(16):
                        ins = e.activation(out=crep[:, c, kc, :], in_=zeros[:], func=AF.Identity, bias=cf[:, c, kc:kc + 1])
                    return ins
                add("act", rep, reads=[b_cf, b_zeros], writes=[b_crep])
            modT, b_modT = sbuf(ph, [128, 96, 2], F32, "modT")
            brow, b_brow = sbuf(ph, [1, 2, 2048], F32, "brow")
            grow, b_grow = sbuf(ph, [128, 256], F32, "grow")
            add("sp", lambda e: e.dma_start(out=brow[0:1, 0, :], in_=b_ada[0:1, 2 * D:3 * D]), writes=[b_brow], dma=True)
            add("sp", lambda e: e.dma_start(out=brow[0:1, 1, :], in_=b_ada[0:1, 5 * D:6 * D]), writes=[b_brow], dma=True)
            wa = [sbuf(ph, [128, 16, 512], BF16, "wa") for _ in range(3)]
            grows = [sbuf(ph, [128, 512], F32, "grow2") for _ in range(2)]
            w_ada_v = w_ada.rearrange("(p kc) n -> p kc n", kc=16)
            MODBANK = 7
            first_mod = [True]
            def ada_chunk(nb):
                wt, b_wt = wa[nb % 3]
                add("pool", lambda e, wt=wt, nb=nb: e.dma_start(out=wt[:], in_=w_ada_v[:, :, nb * 512:(nb + 1) * 512]),
                    writes=[b_wt], dma=True)
                v = nb // 4
                if v in (0, 1, 3, 4):
                    def mm(e, wt=wt, nb=nb):
                        ins = None
                        for sub in range(4):
                            for kc in range(16):
                                ins = e.matmul(pb[1][:, sub * 2:sub * 2 + 2], lhsT=wt[:, kc, sub * 128:(sub + 1) * 128],
                                               rhs=cbf[:, kc, :], start=(kc == 0 and sub == 0), stop=(kc == 15),
                                               skip_group_check=True)
                        return ins
                    add("pe", mm, reads=[b_wt, b_cbf], writes=[b_pb[1]])
                    ch0 = nb * 4
                    for c in range(2):
                        add("dve", lambda e, c=c, ch0=ch0: e.tensor_tensor(
                            out=modT[:, ch0:ch0 + 4, c], in0=pb[1][:, c:8:2], in1=colsT[:, 0, ch0:ch0 + 4], op=ALU.add),
                            reads=[b_pb[1], b_cols], writes=[b_modT])
                else:
                    gi0 = 0 if v == 2 else 2
                    col0 = (nb % 4) * 512
                    for c in range(2):
                        bank = c
                        def mmg(e, wt=wt, c=c, bank=bank):
                            ins = None
                            for kc in range(16):
                                ins = e.matmul(pb[bank], lhsT=crep[:, c, kc, :], rhs=wt[:, kc, :],
                                               start=(kc == 0), stop=(kc == 15))
                            return ins
                        add("pe", mmg, reads=[b_wt, b_crep], writes=[b_pb[bank]])
                        gr, b_gr = grows[c]
                        add("dve", lambda e, bank=bank, v=v, col0=col0, gr=gr: e.tensor_tensor(
                            out=gr[0:1, :], in0=pb[bank][0:1, :], in1=brow[0:1, 0 if v == 2 else 1, col0:col0 + 512], op=ALU.add),
                            reads=[b_pb[bank], b_brow], writes=[b_gr])
                        add("sp", lambda e, gi=gi0 + c, col0=col0, gr=gr: e.dma_start(out=g_scr[gi:gi + 1, col0:col0 + 512], in_=gr[0:1, :]),
                            reads=[b_gr], writes=[b_g], dma=True)

            def ada_AB(ni):
                sh0, sc0, g0 = ((0, 16, 0), (48, 64, 16))[ni]
                for c in range(2):
                    add("dve", lambda e, c=c: e.scalar_tensor_tensor(
                        out=AB[:, ni, c, 0, :], in0=modT[:, sc0:sc0 + 16, c], scalar=1.0, in1=colsT[:, 1, g0:g0 + 16],
                        op0=ALU.add, op1=ALU.mult), reads=[b_modT, b_cols], writes=[b_AB])
                    add("dve", lambda e, c=c: e.tensor_copy(out=AB[:, ni, c, 1, :], in_=modT[:, sh0:sh0 + 16, c]),
                        reads=[b_modT], writes=[b_AB])

            for nb in range(8):
                ada_chunk(nb)
            ada_AB(0)
            ada_pending = list(range(8, 24))

        def run_ada(n):
            for _ in range(n):
                if ada_pending:
                    ada_chunk(ada_pending.pop(0))

        head_hook = [lambda: None]

        def xrows(tok):
            return (xs, tok) if tok < 2048 else (xp, tok - 2048)

        def stage_a(hT, b_hT, n_tok, segs3, a1_hook, after_a1, after_a1_free):
            tok0 = 0
            cache = True
            rope = True
            segs = [(s0_, l_) for (s0_, l_, c_) in segs3]
            nt = n_tok // 128
            nsub = n_tok // 512
            nkeys_max = (512 if cache else 0) + max(l for (_, l) in segs)
            with ExitStack() as ph:
                with ExitStack() as ph1:
                    xts = [sbuf(ph1, [128, D], F32, "xt") for _ in range(2)]
                    xn, b_xn = sbuf(ph1, [128, D], F32, "xn")
                    ss, b_ss = sbuf(ph1, [128, 1], F32, "ss")
                    for tt in range(nt):
                        xt, b_xt = xts[tt % 2]
                        src_, r0_ = xrows(tt * 128)
                        add("sp", lambda e, xt=xt, src_=src_, r0_=r0_: e.dma_start(out=xt[:], in_=src_[r0_:r0_ + 128, :]),
                            writes=[b_xt], dma=True)
                        norm_tile_to_hT(xt, b_xt, 128, xn, b_xn, ss, b_ss, 0, 0 if tt < 16 else 1, hT, b_hT,
                                        lambda kc, tt=tt: hT[:, kc, tt * 128:(tt + 1) * 128], (6, 7))
                        a1_hook()
                    after_a1()
                    S.barrier()
                after_a1_free()
                phh = ExitStack()
                wq = [sbuf(phh, [128, 16, 3, 128], BF16, "wq") for _ in range(2)]
                qf, b_qf = sbuf(phh, [128, 512], F32, "qf")
                qT, b_qT = sbuf(phh, [128, n_tok // 256, 512], BF16, "qT")
                add("dve", lambda e: e.memset(qT[:], 0.0), writes=[b_qT])
                kT, b_kT = sbuf(phh, [128, (512 if cache else 0) + n_tok], BF16, "kT")
                vT, b_vT = sbuf(phh, [128, 512], F32, "vT")
                vtok, b_vtok = sbuf(phh, [128, (4 if cache else 0) + nt, 128], BF16, "vtok")
                kf, b_kf = sbuf(phh, [128, 512], F32, "kf")
                kvo = [sbuf(phh, [128, 4, 128], F32, "kvo") for _ in range(2)]
                kcst, b_kcst = sbuf(phh, [128, 4, 128], F32, "kcst")
                t1, b_t1 = sbuf(phh, [128, 512], F32, "t1")
                t2, b_t2 = sbuf(phh, [128, 512], F32, "t2")
                PTs = [sbuf(phh, [128, 512], BF16, "PT") for _ in range(4)]
                rl, b_rl = sbuf(phh, [128, 512], F32, "rl")
                OL, b_OL = sbuf(phh, [128, 2, 512], F32, "OL")
                tn, b_tn = sbuf(phh, [128, 512], F32, "tn")
                oTs = [sbuf(phh, [128, 256], F32, "oT") for _ in range(2)]
                sqs = [sbuf(phh, [128, 256], F32, "sq") for _ in range(2)]
                fin_i = [0]
                rs, b_rs = sbuf(phh, [128, 256], F32, "rs")
                cst = [sbuf(phh, [128, n_tok], BF16, "cst") for _ in range(2)]
                if rope:
                    cosT, b_cos = sbuf(phh, [128, 2048], F32, "cos")
                    sinT, b_sin = sbuf(phh, [128, 2048], F32, "sin")
                    add("sp", lambda e: e.dma_start(out=cosT[:], in_=cos_d[:, :]), writes=[b_cos], dma=True)
                    add("sp", lambda e: e.dma_start(out=sinT[:], in_=sin_d[:, :]), writes=[b_sin], dma=True)
                koff = 512 if cache else 0
                wq_b = [[Buf("wqb") for _ in range(3)] for _ in range(2)]
                wqp_b = [[Buf("wqpb") for _ in range(2)] for _ in range(2)]
                w_in_v = w_in.rearrange("(kc p) (t h c) -> p kc t h c", p=128, t=4, h=8)
                pt_i = [0]
                st_i = [0]
                pj_i = [0]
                PROJ_BANKS = [0, 2, 3, 4, 5]
                pend = [None]
                catp = [None]
                def load_kc(hd):
                    add("sp", lambda e: e.dma_start(out=kcst[:], in_=ck.rearrange("(t p) f -> p t f", p=128)[:, :, hd * 128:(hd + 1) * 128]),
                        writes=[b_kcst], dma=True)

                def load_head_w(hd):
                    wt_, _b = wq[hd % 2]
                    for t in range(3):
                        add("pool", lambda e, t=t: e.dma_start(out=wt_[:, :, t, :], in_=w_in_v[:, :, t, hd, :]), writes=[wq_b[hd % 2][t]], dma=True)

                fin2_holder = [None]

                def flush_head():
                    if pend[0] is not None:
                        fin2_holder[0](pend[0])
                        pend[0] = None
                    if catp[0] is not None:
                        cs_, b_cs_, hd_ = catp[0]
                        catp[0] = None
                        add("sp", lambda e: e.dma_start(out=cat_scr[hd_, :, tok0:tok0 + n_tok], in_=cs_[:]), reads=[b_cs_], writes=[b_cat], dma=True)

                load_head_w(0)
                for hd in range(8):
                    add("pool", lambda e, hd=hd: e.dma_start(out=vtok[:, 0:4, :], in_=cv.rearrange("(t p) f -> p t f", p=128)[:, :, hd * 128:(hd + 1) * 128]),
                        writes=[b_vtok], dma=True)
                    if hd + 1 < 8:
                        load_head_w(hd + 1)
                    head_hook[0]()
                    wt, _b = wq[hd % 2]
                    b_wts = wq_b[hd % 2]
                    if cache:
                        if hd == 0:
                            load_kc(0)

                        def ktr(e):
                            ins = None
                            for i in range(4):
                                ins = e.transpose(out=pb[1][:, i * 128:(i + 1) * 128], in_=kcst[:, i, :], identity=ident[:])
                            return ins
                        add("pe", ktr, reads=[b_kcst, b_ident], writes=[b_pb[1]])
                        if hd + 1 < 8:
                            load_kc(hd + 1)
                        add("act", lambda e: e.activation(out=kT[:, 0:512], in_=pb[1][:, :], func=AF.Copy), reads=[b_pb[1]], writes=[b_kT])
                    for t, dst, b_dst, doff in ((0, qT, b_qT, 0), (1, kT, b_kT, koff)):
                        for sub in range(nsub):
                            c0 = sub * 512
                            bk = PROJ_BANKS[pj_i[0] % 5]
                            pj_i[0] += 1

                            def mm(e, bk=bk, wt=wt, t=t, c0=c0):
                                ins = None
                                for kc in range(16):
                                    ins = e.matmul(pb[bk][:, :], lhsT=wt[:, kc, t, :], rhs=hT[:, kc, c0:c0 + 512], start=(kc == 0), stop=(kc == 15))
                                return ins
                            add("pe", mm, reads=[b_wts[t], b_hT], writes=[b_pb[bk]])
                            if t == 0 and sub == 3:
                                flush_head()
                            if sub < 4:
                                add("act", lambda e, bk=bk: e.activation(out=qf[:], in_=pb[bk], func=AF.Copy), reads=[b_pb[bk]], writes=[b_qf])
                                add("pe", lambda e, bk=bk: e.matmul(pb[1], lhsT=permT[:, :], rhs=qf[:, :], start=True, stop=True),
                                    reads=[b_qf, b_perm], writes=[b_pb[1]])
                                add("dve", lambda e, bk=bk, c0=c0: e.tensor_tensor(out=t1[:], in0=pb[bk][:, :], in1=cosT[:, c0:c0 + 512], op=ALU.mult),
                                    reads=[b_pb[bk], b_cos], writes=[b_t1])
                                add("dve", lambda e, bk=bk, c0=c0: e.tensor_tensor(out=t2[:], in0=pb[1][:, :], in1=sinT[:, c0:c0 + 512], op=ALU.mult),
                                    reads=[b_pb[1], b_sin], writes=[b_t2])
                                if t == 0:
                                    def qw(e, bk=bk, c0=c0):
                                        s2 = c0 // 256
                                        e.tensor_tensor(out=qT[0:64, s2:s2 + 2, 0:256], in0=t1[0:64, :].rearrange("p (s q) -> p s q", s=2),
                                                        in1=t2[0:64, :].rearrange("p (s q) -> p s q", s=2), op=ALU.add)
                                        return e.tensor_tensor(out=qT[64:128, s2:s2 + 2, 256:512], in0=t1[64:128, :].rearrange("p (s q) -> p s q", s=2),
                                                               in1=t2[64:128, :].rearrange("p (s q) -> p s q", s=2), op=ALU.add)
                                    add("dve", qw, reads=[b_t1, b_t2], writes=[b_qT])
                                else:
                                    add("dve", lambda e, bk=bk, dst=dst, doff=doff, c0=c0: e.tensor_tensor(out=dst[:, doff + c0:doff + c0 + 512], in0=t1[:], in1=t2[:], op=ALU.add),
                                        reads=[b_t1, b_t2], writes=[b_dst])
                            else:
                                if t == 0:
                                    def qw2(e, bk=bk, c0=c0):
                                        s2 = c0 // 256
                                        e.activation(out=qT[0:64, s2:s2 + 2, 0:256], in_=pb[bk][0:64, :].rearrange("p (s q) -> p s q", s=2), func=AF.Copy)
                                        return e.activation(out=qT[64:128, s2:s2 + 2, 256:512], in_=pb[bk][64:128, :].rearrange("p (s q) -> p s q", s=2), func=AF.Copy)
                                    add("act", qw2, reads=[b_pb[bk]], writes=[b_qT])
                                else:
                                    add("act", lambda e, bk=bk, dst=dst, doff=doff, c0=c0: e.activation(out=dst[:, doff + c0:doff + c0 + 512], in_=pb[bk][:, :], func=AF.Copy),
                                        reads=[b_pb[bk]], writes=[b_dst])
                                if sub == 4 and t == 1:
                                    add("dve", lambda e, bk=bk: e.tensor_copy(out=kf[:], in_=pb[bk][:, :]), reads=[b_pb[bk]], writes=[b_kf])
                                    ko, b_ko = kvo[0]

                                    def ktr2(e, bk=bk):
                                        ins = None
                                        for i in range(4):
                                            ins = e.transpose(out=pb[1][:, i * 128:(i + 1) * 128], in_=kf[:, i * 128:(i + 1) * 128], identity=ident[:])
                                        return ins
                                    add("pe", ktr2, reads=[b_kf, b_ident], writes=[b_pb[1]])
                                    add("act", lambda e, bk=bk, ko=ko: e.activation(out=ko[:].rearrange("p a b -> p (a b)"), in_=pb[1][:, :], func=AF.Copy),
                                        reads=[b_pb[1]], writes=[b_ko])
                                    add("sp", lambda e, bk=bk, ko=ko, hd=hd, c0=c0: e.dma_start(
                                        out=sk[c0 - 2048:c0 - 2048 + 512, hd * 128:(hd + 1) * 128].rearrange("(t p) f -> p t f", p=128), in_=ko[:]),
                                        reads=[b_ko, b_out], dma=True)
                    for sub in range(nsub):
                        c0 = sub * 512
                        bk = PROJ_BANKS[pj_i[0] % 5]
                        pj_i[0] += 1

                        def mmv(e, bk=bk, wt=wt, c0=c0):
                            ins = None
                            for kc in range(16):
                                ins = e.matmul(pb[bk][:, :], lhsT=wt[:, kc, 2, :], rhs=hT[:, kc, c0:c0 + 512], start=(kc == 0), stop=(kc == 15))
                            return ins
                        add("pe", mmv, reads=[b_wts[2], b_hT], writes=[b_pb[bk]])
                        add("act", lambda e, bk=bk, c0=c0: e.activation(out=vT[:, :], in_=pb[bk][:, :], func=AF.Copy), reads=[b_pb[bk]], writes=[b_vT])

                        def vtr(e, bk=bk, c0=c0):
                            ins = None
                            for i in range(4):
                                ins = e.transpose(out=pb[1][:, i * 128:(i + 1) * 128], in_=vT[:, i * 128:(i + 1) * 128], identity=ident[:])
                            return ins
                        add("pe", vtr, reads=[b_vT, b_ident], writes=[b_pb[1]])
                        kt0 = (4 if cache else 0) + sub * 4
                        add("act", lambda e, bk=bk, kt0=kt0: e.activation(out=vtok[:, kt0:kt0 + 4, :].rearrange("p a b -> p (a b)"), in_=pb[1][:, :], func=AF.Copy),
                            reads=[b_pb[1]], writes=[b_vtok])
                        if sub == 4:
                            vo, b_vo = kvo[1]
                            add("dve", lambda e, bk=bk, vo=vo: e.tensor_copy(out=vo[:].rearrange("p a b -> p (a b)"), in_=pb[1][:, :]), reads=[b_pb[1]], writes=[b_vo])
                            add("sp", lambda e, bk=bk, vo=vo, hd=hd, c0=c0: e.dma_start(
                                out=sv[c0 - 2048:c0 - 2048 + 512, hd * 128:(hd + 1) * 128].rearrange("(t p) f -> p t f", p=128), in_=vo[:]),
                                reads=[b_vo, b_out], dma=True)
                    cs, b_cs = cst[hd % 2]
                    def fin2(args):
                        o_, b_o, q_, b_q, q0, cs_, b_cs_ = args
                        add("pe", lambda e: e.matmul(pb[1][:, 0:256], lhsT=ones_f[:, :], rhs=q_[:, :], start=True, stop=True),
                            reads=[b_q, b_ones_f], writes=[b_pb[1]])
                        add("act", lambda e: e.activation(out=rs[:], in_=pb[1][:, 0:256], func=AF.Ln, scale=1.0 / 128, bias=epsc[:, 0:1]),
                            reads=[b_pb[1], b_epsc], writes=[b_rs])
                        add("act", lambda e: e.activation(out=rs[:], in_=rs[:], func=AF.Exp, scale=-0.5), reads=[b_rs], writes=[b_rs])
                        add("dve", lambda e: e.scalar_tensor_tensor(out=cs_[:, q0:q0 + 256], in0=o_[:], scalar=gsub[:, 0:1], in1=rs[:],
                                                                   op0=ALU.mult, op1=ALU.mult), reads=[b_o, b_rs, b_gsub], writes=[b_cs_])
                    fin2_holder[0] = fin2

                    for (s0, slen, segc) in segs3:
                        nkc = ((512 if segc else 0) + slen) // 128
                        for qs in range(slen // 256):
                            q0 = s0 + qs * 256
                            qsi = q0 // 256
                            its = []
                            for kc in range(nkc):
                                if segc:
                                    kcol = kc * 128 if kc < 4 else koff + s0 + (kc - 4) * 128
                                    vti = kc if kc < 4 else 4 + (s0 // 128) + (kc - 4)
                                else:
                                    kcol = koff + s0 + kc * 128
                                    vti = 4 + (s0 // 128) + kc
                                sA = 2 + (st_i[0] % 4)
                                st_i[0] += 1
                                PT, b_PT = PTs[pt_i[0] % 4]
                                pt_i[0] += 1
                                its.append((kc, kcol, vti, sA, PT, b_PT))

                            def emit_qk(it, qsi=qsi):
                                kc, kcol, vti, sA, PT, b_PT = it
                                add("pe", lambda e: e.matmul(pb[sA], lhsT=kT[:, kcol:kcol + 128], rhs=qT[:, qsi, :], start=True, stop=True),
                                    reads=[b_kT, b_qT], writes=[b_pb[sA]])

                            def emit_ex_pv(it, nxt=None, nkc=nkc, qsi=qsi):
                                kc, kcol, vti, sA, PT, b_PT = it
                                add("act", lambda e: e.activation(out=PT[:, :], in_=pb[sA], func=AF.Exp, scale=QSCALE), reads=[b_pb[sA]], writes=[b_PT])

                                def pv(e):
                                    e.matmul(pb[6], lhsT=vtok[:, vti, :], rhs=PT[:, :], start=(kc == 0), stop=(kc == nkc - 1))
                                    ins = e.matmul(pb[7], lhsT=ones_b[:, :], rhs=PT[:, :], start=(kc == 0), stop=(kc == nkc - 1))
                                    if nxt is not None:
                                        ins = e.matmul(pb[nxt[3]], lhsT=kT[:, nxt[1]:nxt[1] + 128], rhs=qT[:, qsi, :], start=True, stop=True)
                                    return ins
                                wr = [b_pb[6], b_pb[7]] + ([b_pb[nxt[3]]] if nxt is not None else [])
                                add("pe", pv, reads=[b_PT, b_vtok, b_ones_b, b_kT, b_qT], writes=wr)

                            LA = 3
                            for ii in range(min(LA, len(its))):
                                emit_qk(its[ii])
                            for ii in range(len(its)):
                                emit_ex_pv(its[ii], its[ii + LA] if ii + LA < len(its) else None)
                                if ii == min(12, len(its) - 1) and pend[0] is not None:
                                    fin2(pend[0])
                                    pend[0] = None
                            if pend[0] is not None:
                                fin2(pend[0])
                                pend[0] = None
                            o_, b_o = oTs[fin_i[0] % 2]
                            q_, b_q = sqs[fin_i[0] % 2]
                            fin_i[0] += 1
                            add("dve", lambda e: e.tensor_copy(out=OL[:, :, :], in_=pball[:, 6:8, :]), reads=[b_pb[6], b_pb[7]], writes=[b_OL])
                            add("dve", lambda e: e.reciprocal(out=rl[:], in_=OL[:, 1, :]), reads=[b_OL], writes=[b_rl])
                            add("dve", lambda e: e.tensor_tensor(out=tn[:], in0=OL[:, 0, :], in1=rl[:], op=ALU.mult), reads=[b_OL, b_rl], writes=[b_tn])
                            add("dve", lambda e, o_=o_: e.scalar_tensor_tensor(out=o_[:], in0=tn[:, 256:512], scalar=neglam[:, 0:1], in1=tn[:, 0:256],
                                                                              op0=ALU.mult, op1=ALU.add), reads=[b_tn, b_neglam], writes=[b_o])
                            add("dve", lambda e, o_=o_, q_=q_: e.tensor_tensor(out=q_[:], in0=o_[:], in1=o_[:], op=ALU.mult), reads=[b_o], writes=[b_q])
                            pend[0] = (o_, b_o, q_, b_q, q0, cs, b_cs)
                    catp[0] = (cs, b_cs, hd)
                flush_head()
                S.barrier()
                phh.close()
                with ExitStack() as ph2:
                    Wd = sum(l + 16 for (_, l) in segs)
                    pp = [sbuf(ph2, [128, Wd], F32, "pp") for _ in range(2)]
                    sa, b_sa = sbuf(ph2, [128, Wd], F32, "sa")
                    sb2, b_sb2 = sbuf(ph2, [128, Wd], F32, "sb")
                    pooled, b_pooled = sbuf(ph2, [128, 2, n_tok], BF16, "pooled")
                    e8, b_e8 = sbuf(ph2, [128, 8], F32, "e8")
                    wpl = [sbuf(ph2, [128, 16, 2, 128], BF16, "wpl") for _ in range(2)]
                    wpp = [sbuf(ph2, [128, 2, 256], BF16, "wpp") for _ in range(2)]
                    cst2 = [sbuf(ph2, [128, n_tok], BF16, "cst2") for _ in range(2)]
                    for (p_, b_p) in pp:
                        add("dve", lambda e, p_=p_: e.memset(p_[:], 0.0), writes=[b_p])
                    w_in_p = w_in.rearrange("(kc p) (t g c) -> p kc t g c", p=128, t=4, g=8)
                    wpl_b = [[Buf("wplb") for _ in range(2)] for _ in range(2)]
                    def load_pool_w(g):
                        wt_, _b = wpl[g % 2]
                        for chl in range(2):
                            add("pool", lambda e, chl=chl: e.dma_start(out=wt_[:, :, chl, :], in_=w_in_p[:, :, 3, 2 * g + chl, :]), writes=[wpl_b[g % 2][chl]], dma=True)
                        wp__, b_wp_ = wpp[g % 2]
                        add("pool", lambda e: e.dma_start(out=wp__[:], in_=w_pool[g].rearrange("(kc p) n -> p kc n", p=128)), writes=[b_wp_], dma=True)

                    pp_i = [0]
                    pw_i = [0]
                    load_pool_w(0)
                    for g in range(4):
                        w = POOLW[g]
                        wt, _b = wpl[g % 2]
                        b_wls = wpl_b[g % 2]
                        wp_, b_wp = wpp[g % 2]
                        if g + 1 < 4:
                            load_pool_w(g + 1)
                        for chl in range(2):
                            p_, b_p = pp[chl]
                            for sub in range(nsub):
                                c0 = sub * 512

                                bkp = (0, 2, 3, 4, 5)[pp_i[0] % 5]
                                pp_i[0] += 1

                                def mmp2(e, wt=wt, chl=chl, c0=c0, bkp=bkp):
                                    ins = None
                                    for kc in range(16):
                                        ins = e.matmul(pb[bkp], lhsT=wt[:, kc, chl, :], rhs=hT[:, kc, c0:c0 + 512], start=(kc == 0), stop=(kc == 15))
                                    return ins
                                add("pe", mmp2, reads=[b_wls[chl], b_hT], writes=[b_pb[bkp]])
                                off = 0
                                for (s0, slen) in segs:
                                    lo = max(s0, c0); hi = min(s0 + slen, c0 + 512)
                                    if lo < hi:
                                        po = off + 8 + (lo - s0)
                                        add("act", lambda e, p_=p_, po=po, lo=lo, hi=hi, c0=c0, bkp=bkp: e.activation(
                                            out=p_[:, po:po + hi - lo], in_=pb[bkp][:, lo - c0:hi - c0], func=AF.Copy),
                                            reads=[b_pb[bkp]], writes=[b_p])
                                    off += slen + 16
                            srcs = [(p_, b_p), (sa, b_sa), (sb2, b_sb2)]
                            cur, b_cur = p_, b_p
                            sh = 1
                            k = 0
                            while sh < w:
                                dstt, b_dstt = (sa, b_sa) if k % 2 == 0 else (sb2, b_sb2)
                                nout = Wd - (2 * sh - 1)
                                add("dve", lambda e, cur=cur, dstt=dstt, sh=sh, nout=nout: e.tensor_tensor(
                                    out=dstt[:, 0:nout], in0=cur[:, 0:nout], in1=cur[:, sh:nout + sh], op=ALU.add),
                                    reads=[b_cur], writes=[b_dstt])
                                cur, b_cur = dstt, b_dstt
                                sh *= 2
                                k += 1
                            off = 0
                            for (s0, slen) in segs:
                                u0 = off + 8
                                h2_ = w // 2
                                add("dve", lambda e, cur=cur, p_=p_, u0=u0, slen=slen, s0=s0, chl=chl, w=w, h2_=h2_: e.scalar_tensor_tensor(
                                    out=pooled[:, chl, s0:s0 + slen], in0=cur[:, u0 - h2_:u0 - h2_ + slen], scalar=1.0 / w, in1=p_[:, u0:u0 + slen],
                                    op0=ALU.mult, op1=ALU.subtract), reads=[b_cur, b_p], writes=[b_pooled])
                                for side in range(2):
                                    ub = u0 if side == 0 else u0 + slen - 8
                                    tb = s0 if side == 0 else s0 + slen - 8
                                    ecol = g * 16 + side * 8
                                    add("dve", lambda e, cur=cur, ub=ub, h2_=h2_, ecol=ecol: e.tensor_tensor(
                                        out=e8[:], in0=cur[:, ub - h2_:ub - h2_ + 8], in1=edge[:, ecol:ecol + 8], op=ALU.mult),
                                        reads=[b_cur, b_edge], writes=[b_e8])
                                    add("dve", lambda e, p_=p_, ub=ub, tb=tb, chl=chl: e.tensor_tensor(
                                        out=pooled[:, chl, tb:tb + 8], in0=e8[:], in1=p_[:, ub:ub + 8], op=ALU.subtract),
                                        reads=[b_e8, b_p], writes=[b_pooled])
                                off += slen + 16
                        for oc2 in range(2):
                            cs, b_cs = cst2[oc2]
                            for sub in range(nsub):
                                c0 = sub * 512

                                bkw = (1, 6, 7)[pw_i[0] % 3]
                                pw_i[0] += 1

                                def mmw(e, wp_=wp_, oc2=oc2, c0=c0, bkw=bkw):
                                    ins = None
                                    for kc2 in range(2):
                                        ins = e.matmul(pb[bkw], lhsT=wp_[:, kc2, oc2 * 128:(oc2 + 1) * 128], rhs=pooled[:, kc2, c0:c0 + 512],
                                                       start=(kc2 == 0), stop=(kc2 == 1))
                                    return ins
                                add("pe", mmw, reads=[b_wp, b_pooled], writes=[b_pb[bkw]])
                                pc = 32 + 2 * g + oc2
                                add("act", lambda e, cs=cs, c0=c0, pc=pc, bkw=bkw: e.activation(out=cs[:, c0:c0 + 512], in_=pb[bkw], func=AF.Identity,
                                                                                              scale=colsT[:, 1, pc:pc + 1]), reads=[b_pb[bkw], b_cols], writes=[b_cs])
                            add("sp", lambda e, cs=cs, g=g, oc2=oc2: e.dma_start(out=cat_scr[8 + 2 * g + oc2, :, tok0:tok0 + n_tok], in_=cs[:]),
                                reads=[b_cs], writes=[b_cat], dma=True)
                    S.barrier()
            S.barrier()

        def _after_a1():
            run_ada(100)
            ada_AB(1)

        head_hook[0] = lambda: issue_precast(14)
        stage_a(hT_all, b_hT_all, NTOK, [(0, 2048, True), (2048, 256, False), (2304, 256, False)],
                lambda: run_ada(1), _after_a1, lambda: ph0.close())
        hT_scope.close()

        issue_precast(1000)
        with ExitStack() as ph:
            catb = [sbuf(ph, [128, 16, 512], BF16, "catb") for _ in range(2)]
            h2st = [sbuf(ph, [128, 16, 512], BF16, "h2st") for _ in range(2)]
            wo, _b = sbuf(ph, [128, 16, D], BF16, "wo")
            b_wo = [Buf("wo%d" % oc) for oc in range(4)]
            xts = [sbuf(ph, [128, D], F32, "xt4") for _ in range(3)]
            xn4s = [sbuf(ph, [128, D], F32, "xn4") for _ in range(2)]
            ss4s = [sbuf(ph, [128, 1], F32, "ss4") for _ in range(2)]
            g1r = [sbuf(ph, [128, D], F32, "g1r") for _ in range(2)]
            tmp = [sbuf(ph, [128, 512], F32, "tmp4") for _ in range(2)]
            def load_g1(c):
                add("act", lambda e: e.dma_start(out=g1r[c][0][:], in_=g_scr[c:c + 1, :].partition_broadcast(128)),
                    reads=[b_g], writes=[g1r[c][1]], dma=True)

            def load_wo(oc):
                add("act", lambda e: e.dma_start(out=wo[:, :, oc * 512:(oc + 1) * 512], in_=wo_scr[:, :, oc * 512:(oc + 1) * 512]),
                    reads=[b_wos[oc]], writes=[b_wo[oc]], dma=True)

            load_wo(0)
            load_g1(1)
            load_wo(1)
            load_wo(2)
            load_wo(3)
            load_g1(0)

            ti = 0
            mi = 0
            order = [4, 0, 1, 2, 3]
            tiles = [(blk, tt) for blk in order for tt in range(4)]

            def load_cat(oi):
                blk = order[oi]
                cb, b_cb = catb[oi % 2]
                t0 = blk * 512
                add("sp", lambda e: e.dma_start(out=cb[:], in_=cat_scr[:, :, t0:t0 + 512].rearrange("k p t -> p k t")),
                    reads=[b_cat], writes=[b_cb], dma=True)

            def load_x(n):
                blk, tt = tiles[n]
                xt, b_xt = xts[n % 3]
                src, r0 = xrows(blk * 512 + tt * 128)
                add("sp", lambda e: e.dma_start(out=xt[:], in_=src[r0:r0 + 128, :]), writes=[b_xt], dma=True)

            def do_stats(n):
                xt, b_xt = xts[n % 3]
                xn4, b_xn4 = xn4s[n % 2]
                ss4, b_ss4 = ss4s[n % 2]
                norm_stats(xt, b_xt, 128, xn4, b_xn4, ss4, b_ss4)

            def do_norm(n):
                blk, tt = tiles[n]
                oi = n // 4
                xn4, b_xn4 = xn4s[n % 2]
                hs, b_hs = h2st[oi % 2]
                cond = 0 if blk < 4 else 1
                norm_transposes(128, xn4, b_xn4, 1, cond, b_hs, lambda kc: hs[:, kc, tt * 128:(tt + 1) * 128], (6, 7))
                if tt == 3:
                    t0 = blk * 512
                    add("pool", lambda e: e.dma_start(out=h2_scr[:, :, t0:t0 + 512].rearrange("k p t -> p k t"), in_=hs[:]),
                        reads=[b_hs], writes=[b_h2s], dma=True)

            load_cat(0)
            load_x(0)
            load_x(1)
            for n, (blk, tt) in enumerate(tiles):
                oi = n // 4
                t0 = blk * 512
                cond = 0 if blk < 4 else 1
                cb, b_cb = catb[oi % 2]
                if tt == 0 and oi + 1 < 5:
                    load_cat(oi + 1)
                xt, b_xt = xts[n % 3]
                for oc in range(4):
                    bank = mi % 6
                    mi += 1

                    def mmo(e, cb=cb, tt=tt, bank=bank, oc=oc):
                        ins = None
                        for kc in range(16):
                            ins = e.matmul(pb[bank], lhsT=cb[:, kc, tt * 128:(tt + 1) * 128], rhs=wo[:, kc, oc * 512:(oc + 1) * 512],
                                           start=(kc == 0), stop=(kc == 15))
                        return ins
                    add("pe", mmo, reads=[b_cb, b_wo[oc]], writes=[b_pb[bank]])
                    tm, b_tm = tmp[ti % 2]
                    ti += 1
                    add("dve", lambda e, tm=tm, bank=bank, cond=cond, oc=oc: e.tensor_tensor(
                        out=tm[:], in0=pb[bank], in1=g1r[cond][0][:, oc * 512:(oc + 1) * 512], op=ALU.mult),
                        reads=[b_pb[bank], g1r[cond][1]], writes=[b_tm])
                    add("dve", lambda e, tm=tm, xt=xt, oc=oc: e.tensor_tensor(
                        out=xt[:, oc * 512:(oc + 1) * 512], in0=xt[:, oc * 512:(oc + 1) * 512], in1=tm[:], op=ALU.add),
                        reads=[b_tm, b_xt], writes=[b_xt])
                r1 = t0 + tt * 128
                add("pool", lambda e, xt=xt, r1=r1: e.dma_start(out=x1_scr[r1:r1 + 128, :], in_=xt[:]), reads=[b_xt], writes=[b_x1], dma=True)
                if n >= 1:
                    do_norm(n - 1)
                do_stats(n)
                if n + 2 < len(tiles):
                    load_x(n + 2)
            do_norm(len(tiles) - 1)
            S.barrier()

        with ExitStack() as ph:
            h2Ts = [sbuf(ph, [128, 16, 512], BF16, "h2T") for _ in range(2)]
            hH, b_hH = sbuf(ph, [128, 16, 6], BF16, "hH")
            uh_g, b_uhg = sbuf(ph, [128, NJ, 8], F32, "uh_g")
            uh_v, b_uhv = sbuf(ph, [128, NJ, 8], F32, "uh_v")
            gT, b_gT = sbuf(ph, [128, NJ, 512], BF16, "gT")
            wu = [sbuf(ph, [128, 16, 2, 128], BF16, "wu") for _ in range(2)]
            wd = [sbuf(ph, [128, 11, 512], BF16, "wd") for _ in range(2)]
            xts = [sbuf(ph, [128, D], F32, "xtb") for _ in range(4)]
            xn, b_xn = sbuf(ph, [128, D], BF16, "junkb")
            ss, b_ss = sbuf(ph, [128, 1], F32, "ssb")
            ug = [sbuf(ph, [128, 514], F32, "ug") for _ in range(2)]
            uv = [sbuf(ph, [128, 514], F32, "uv") for _ in range(2)]
            tg, b_tg = sbuf(ph, [128, 512], F32, "tg")
            tv, b_tv = sbuf(ph, [128, 512], F32, "tv")
            sg, b_sg = sbuf(ph, [128, 512], F32, "sg")
            fx, b_fx = sbuf(ph, [128, 1], F32, "fx")
            g2rs = [sbuf(ph, [128, D], F32, "g2r") for _ in range(2)]
            gfr, b_gfr = sbuf(ph, [128, D], F32, "gfr")
            tmp = [sbuf(ph, [128, 512], F32, "tmpb") for _ in range(2)]
            add("sp", lambda e: e.dma_start(out=gfr[:], in_=norm_f_g[0:1, :].partition_broadcast(128)), writes=[b_gfr], dma=True)
            for c in range(2):
                add("sp", lambda e, c=c: e.dma_start(out=g2rs[c][0][:], in_=g_scr[2 + c:3 + c, :].partition_broadcast(128)),
                    reads=[b_g], writes=[g2rs[c][1]], dma=True)
            add("dve", lambda e: e.memset(uh_g[:], 0.0), writes=[b_uhg])
            add("dve", lambda e: e.memset(uh_v[:], 0.0), writes=[b_uhv])
            HALO = [511, 512, 1023, 1024, 1535, 1536]
            for i in range(3):
                r0 = HALO[2 * i]
                add("sp", lambda e, r0=r0, i=i: e.dma_start(out=hH[:, :, 2 * i:2 * i + 2], in_=h2_scr[:, :, r0:r0 + 2].rearrange("k p t -> p k t")),
                    reads=[b_h2s], writes=[b_hH], dma=True)
            wdi = 0
            ti = 0
            blocks = [(2048, 1, 0, 0, [(0, 256), (256, 256)], yp, 0)]
            for i in range(4):
                t0 = i * 512
                hlft = 0 if i == 0 else 1 + HALO.index(t0 - 1)
                hrgt = 0 if i == 3 else 1 + HALO.index(t0 + 512)
                blocks.append((t0, 0, hlft, hrgt, [(0, 512)], ys, t0))

            def load_h2(bi):
                t0_ = blocks[bi][0]
                h_, b_h = h2Ts[bi % 2]
                add("sp", lambda e: e.dma_start(out=h_[:], in_=h2_scr[:, :, t0_:t0_ + 512].rearrange("k p t -> p k t")),
                    reads=[b_h2s], writes=[b_h], dma=True)

            load_h2(0)
            for bi, (t0, cond, hlft, hrgt, segs, ydst, y0) in enumerate(blocks):
                h2T, b_h2T = h2Ts[bi % 2]
                g2r, b_g2r = g2rs[cond]
                for tt in range(4):
                    xt, b_xt = xts[tt]
                    r0 = t0 + tt * 128
                    add("pool", lambda e, xt=xt, r0=r0: e.dma_start(out=xt[:], in_=x1_scr[r0:r0 + 128, :]), reads=[b_x1], writes=[b_xt], dma=True)
                for j in range(NJ):
                    if j == 24 and bi + 1 < len(blocks):
                        load_h2(bi + 1)
                    wt, b_wt1 = wu[j % 2]
                    add("sp", lambda e, wt=wt, j=j: e.dma_start(out=wt[:], in_=wup_scr[j]), reads=b_wups[j], writes=[b_wt1], dma=True)
                    st = 2 * (j % 3)
                    bG, bV, bH = st, st + 1, 7
                    u_g, b_ug = ug[j % 2]
                    u_v, b_uv = uv[j % 2]

                    def mmu(e, wt=wt, bG=bG, bV=bV, h2T=h2T):
                        ins = None
                        for kc in range(16):
                            e.matmul(pb[bG], lhsT=wt[:, kc, 0, :], rhs=h2T[:, kc, :], start=(kc == 0), stop=(kc == 15))
                        for kc in range(16):
                            ins = e.matmul(pb[bV], lhsT=wt[:, kc, 1, :], rhs=h2T[:, kc, :], start=(kc == 0), stop=(kc == 15))
                        return ins
                    add("pe", mmu, reads=[b_wt1, b_h2T], writes=[b_pb[bG], b_pb[bV]])
                    if bi == 0:
                        def mmh(e, wt=wt, bH=bH):
                            ins = None
                            for kc in range(16):
                                e.matmul(pb[bH][:, 0:6], lhsT=wt[:, kc, 0, :], rhs=hH[:, kc, :], start=(kc == 0), stop=(kc == 15))
                            for kc in range(16):
                                ins = e.matmul(pb[bH][:, 6:12], lhsT=wt[:, kc, 1, :], rhs=hH[:, kc, :], start=False, stop=(kc == 15),
                                               skip_group_check=True)
                            return ins
                        add("pe", mmh, reads=[b_wt1, b_hH], writes=[b_pb[bH]])

                        def evh(e, j=j, bH=bH):
                            e.activation(out=uh_g[:, j, 1:7], in_=pb[bH][:, 0:6], func=AF.Copy)
                            return e.activation(out=uh_v[:, j, 1:7], in_=pb[bH][:, 6:12], func=AF.Copy)
                        add("act", evh, reads=[b_pb[bH]], writes=[b_uhg, b_uhv])

                    def evu(e, u_g=u_g, u_v=u_v, bG=bG, bV=bV, j=j, hlft=hlft, hrgt=hrgt):
                        e.activation(out=u_g[:, 0:1], in_=uh_g[:, j, hlft:hlft + 1], func=AF.Copy)
                        e.activation(out=u_g[:, 513:514], in_=uh_g[:, j, hrgt:hrgt + 1], func=AF.Copy)
                        e.activation(out=u_v[:, 0:1], in_=uh_v[:, j, hlft:hlft + 1], func=AF.Copy)
                        e.activation(out=u_v[:, 513:514], in_=uh_v[:, j, hrgt:hrgt + 1], func=AF.Copy)
                        e.activation(out=u_g[:, 1:513], in_=pb[bG], func=AF.Copy)
                        return e.activation(out=u_v[:, 1:513], in_=pb[bV], func=AF.Copy)
                    add("act", evu, reads=[b_pb[bG], b_pb[bV], b_uhg, b_uhv], writes=[b_ug, b_uv])
                    for (u_, b_u, t_, b_t, cj) in ((u_g, b_ug, tg, b_tg, j), (u_v, b_uv, tv, b_tv, NJ + j)):
                        def conv(e, u_=u_, t_=t_, cj=cj, segs=segs):
                            e.tensor_scalar(out=t_[:], in0=u_[:, 1:513], scalar1=colsT[:, 3, cj:cj + 1], scalar2=colsT[:, 1, 40 + cj:41 + cj],
                                            op0=ALU.mult, op1=ALU.add)
                            e.scalar_tensor_tensor(out=t_[:], in0=u_[:, 0:512], scalar=colsT[:, 2, cj:cj + 1], in1=t_[:], op0=ALU.mult, op1=ALU.add)
                            ins = e.scalar_tensor_tensor(out=t_[:], in0=u_[:, 2:514], scalar=colsT[:, 4, cj:cj + 1], in1=t_[:], op0=ALU.mult, op1=ALU.add)
                            if len(segs) == 2:
                                e.scalar_tensor_tensor(out=t_[:, 255:256], in0=u_[:, 257:258], scalar=colsT[:, 6, cj:cj + 1], in1=t_[:, 255:256],
                                                       op0=ALU.mult, op1=ALU.add)
                                ins = e.scalar_tensor_tensor(out=t_[:, 256:257], in0=u_[:, 256:257], scalar=colsT[:, 5, cj:cj + 1], in1=t_[:, 256:257],
                                                             op0=ALU.mult, op1=ALU.add)
                            return ins
                        add("dve", conv, reads=[b_u, b_cols], writes=[b_t, b_fx])
                    add("act", lambda e: e.activation(out=sg[:], in_=tg[:], func=AF.Silu), reads=[b_tg], writes=[b_sg])
                    add("dve", lambda e, j=j: e.tensor_tensor(out=gT[:, j, :], in0=sg[:], in1=tv[:], op=ALU.mult), reads=[b_sg, b_tv], writes=[b_gT])
                for oc in range(4):
                    base = 4 * (oc % 2)
                    for fg in range(4):
                        f0 = fg * 11
                        nf = min(11, NJ - f0)
                        wt, b_wt = wd[wdi % 2]
                        wdi += 1
                        add("sp", lambda e, wt=wt, f0=f0, nf=nf, oc=oc: e.dma_start(out=wt[:, 0:nf, :], in_=wdn_scr[oc, :, f0:f0 + nf, :]),
                            reads=[b_wdns[oc][fg]], writes=[b_wt], dma=True)

                        def mmd(e, wt=wt, f0=f0, nf=nf, base=base):
                            ins = None
                            for fl in range(nf):
                                ffc = f0 + fl
                                for tt in range(4):
                                    ins = e.matmul(pb[base + tt][:, :], lhsT=gT[:, ffc, tt * 128:(tt + 1) * 128], rhs=wt[:, fl, :],
                                                   start=(ffc == 0), stop=(ffc == NJ - 1))
                            return ins
                        add("pe", mmd, reads=[b_wt, b_gT], writes=[b_pb[base + tt] for tt in range(4)])
                    for tt in range(4):
                        xt, b_xt = xts[tt]
                        tm, b_tm = tmp[ti % 2]
                        ti += 1
                        add("dve", lambda e, tm=tm, base=base, tt=tt, oc=oc, g2r=g2r: e.tensor_tensor(
                            out=tm[:], in0=pb[base + tt][:, :], in1=g2r[:, oc * 512:(oc + 1) * 512], op=ALU.mult),
                            reads=[b_pb[base + tt], b_g2r], writes=[b_tm])
                        add("dve", lambda e, tm=tm, xt=xt, oc=oc: e.tensor_tensor(
                            out=xt[:, oc * 512:(oc + 1) * 512], in0=xt[:, oc * 512:(oc + 1) * 512], in1=tm[:], op=ALU.add),
                            reads=[b_tm, b_xt], writes=[b_xt])
                for tt in range(4):
                    xt, b_xt = xts[tt]
                    add("act", lambda e, xt=xt: e.activation(out=xn[:], in_=xt[:], func=AF.Square, accum_out=ss[:, 0:1]), reads=[b_xt], writes=[b_xn, b_ss])
                    rstd_from_ss(ss, b_ss, 128, 1.0 / D)
                    add("dve", lambda e, xt=xt: e.scalar_tensor_tensor(out=xt[:], in0=xt[:], scalar=ss[:, 0:1], in1=gfr[:], op0=ALU.mult, op1=ALU.mult),
                        reads=[b_xt, b_ss, b_gfr], writes=[b_xt])
                    r0 = y0 + tt * 128
                    add("pool", lambda e, xt=xt, ydst=ydst, r0=r0: e.dma_start(out=ydst[r0:r0 + 128, :], in_=xt[:]), reads=[b_xt, b_out], dma=True)
            S.barrier()
        add("sp", None, writes=[b_out])
        with nc.Block() as block:
            S.finalize_and_emit(block)
    return nc


_CACHE = {}


def _consts():
    rows = 2048 // 64
    row = np.repeat(np.arange(rows), 64).astype(np.float32)
    col = np.tile(np.arange(64), rows).astype(np.float32)
    inv = (1.0 / (10000.0 ** (np.arange(0, 32, 2, dtype=np.float32) / np.float32(32)))).astype(np.float32)
    ar = row[:, None] * inv
    ac = col[:, None] * inv
    ang = np.concatenate([ar, ar, ac, ac], axis=-1).astype(np.float32)
    cos = np.cos(ang).astype(np.float32)
    sin = np.sin(ang).astype(np.float32)
    sgn = np.ones(64, np.float32)
    for a in range(2):
        sgn[a * 32:a * 32 + 16] = -1.0
    sins = sin * sgn[None, :]
    cosT = np.ascontiguousarray(np.concatenate([cos.T, cos.T], axis=0))
    sinT = np.ascontiguousarray(np.concatenate([sins.T, sins.T], axis=0))
    edge = np.zeros((1, 64), np.float32)
    for g, w in enumerate(POOLW):
        for i in range(8):
            t = i
            cl = (t + w - w // 2) - max(t - w // 2, 0)
            edge[0, g * 16 + i] = 1.0 / cl
            tr = -8 + i
            cr = min(0, tr + w - w // 2) - (tr - w // 2)
            edge[0, g * 16 + 8 + i] = 1.0 / cr
    return cosT, sinT, edge


def kernel(x_prompt, x_sample, c, cache_k, cache_v, c_ctx, w_ada, b_ada, norm1_g, w_in,
           lam_q1, lam_k1, lam_q2, lam_k2, subln_g, w_pool, pool_scale, w_out, norm2_g,
           w_up, conv_k, conv_b, w_down, norm_f_g):
    f = lambda a: np.ascontiguousarray(np.asarray(a, dtype=np.float32))
    x_prompt, x_sample, c, cache_k, cache_v, c_ctx = map(f, (x_prompt, x_sample, c, cache_k, cache_v, c_ctx))
    if "nc" not in _CACHE:
        _CACHE["nc"] = build_program()
    nc = _CACHE["nc"]
    cosT, sinT, edge = _consts()
    w_in0 = f(w_in)[0]
    perm = np.zeros((128, 128), np.float32)
    for p in range(128):
        perm[p ^ 16, p] = 1.0
    lamv = np.concatenate([f(lam_q1)[0], f(lam_k1)[0], f(lam_q2)[0], f(lam_k2)[0]])[None, :]
    shared = {
        "w_ada": f(w_ada)[0], "b_ada": f(b_ada), "norm1_g": f(norm1_g), "w_in": w_in0, "perm": perm,
        "lamv": f(lamv), "subln_g": f(subln_g), "w_pool": f(w_pool)[0], "pool_scale": f(pool_scale),
        "w_out": f(w_out)[0], "norm2_g": f(norm2_g), "w_up": f(w_up)[0], "conv_k": f(conv_k)[0], "conv_b": f(conv_b),
        "w_down": f(w_down)[0], "norm_f_g": f(norm_f_g)[None, :], "ident": np.eye(128, dtype=np.float32),
        "cosT": cosT, "sinT": sinT, "edge": edge,
    }
    in_maps = []
    for b in range(8):
        m = dict(shared)
        m["xs"] = x_sample[b]
        m["xp"] = np.ascontiguousarray(x_prompt[2 * b:2 * b + 2].reshape(512, 2048))
        cc = np.stack([c[b].reshape(128, 16), c_ctx.reshape(128, 16)], axis=1)
        m["cT"] = np.ascontiguousarray(cc)
        m["ck"] = np.ascontiguousarray(cache_k[b, 0].reshape(512, 1024))
        m["cv"] = np.ascontiguousarray(cache_v[b, 0].reshape(512, 1024))
        in_maps.append(m)
    res = run_bass_kernel_spmd(nc, in_maps, core_ids=list(range(8)))
    r = res.results
    y_prompt = np.concatenate([r[b]["yp"].reshape(2, 256, 2048) for b in range(8)], axis=0)
    y_sample = np.stack([r[b]["ys"] for b in range(8)], axis=0)
    state_k = np.concatenate([r[b]["sk"].reshape(2, 1, 256, 8, 128) for b in range(8)], axis=0)
    state_v = np.concatenate([r[b]["sv"].reshape(2, 1, 256, 8, 128) for b in range(8)], axis=0)
    return (y_prompt.astype(np.float32), y_sample.astype(np.float32), state_k.astype(np.float32), state_v.astype(np.float32))
```
